# Optimizing a Trainium2 kernel written in Bass

```python
import math
import jax, jax.numpy as jnp
from jax import lax
import numpy as np

D_MODEL = 2048
BATCH = 8
SEQ = 2048
DEPTH = 1

CHUNK = 64
Q_BLOCK = 128
D_SSM = D_MODEL // 2
SSD_HEADDIM = 64
SSD_HEADS = D_SSM // SSD_HEADDIM
SSD_GROUPS = 2
SSD_STATE = 128
SSD_CONV = 4
SSD_CONV_DIM = D_SSM + 2 * SSD_GROUPS * SSD_STATE
MLA_V = 128
MLA_HEADS = (D_MODEL - D_SSM) // MLA_V
MLA_NOPE = 128
MLA_ROPE = 64
MLA_Q_RANK = 512
MLA_KV_RANK = 256
ROPE_THETA = 10000.0
D_IN_PROJ = D_SSM + SSD_CONV_DIM + SSD_HEADS + MLA_Q_RANK + MLA_KV_RANK + MLA_ROPE
D_FF = 5632
FFN_CONV = 3
PLE_DIM = 256
NORM_EPS = 1e-6

kernel_name = "hybrid_ssd_mla_convffn_ple_block"


def rms_norm(x, w):
    xf = x.astype(jnp.float32)
    y = xf * lax.rsqrt(jnp.mean(xf * xf, axis=-1, keepdims=True) + NORM_EPS)
    return y.astype(x.dtype) * w


def causal_dwconv(u, w, b):
    k = w.shape[0]
    out = lax.conv_general_dilated(
        u, w[:, None, :].astype(u.dtype), window_strides=(1,), padding=[(k - 1, 0)],
        dimension_numbers=("NWC", "WIO", "NWC"), feature_group_count=u.shape[-1])
    return out + b


def ssd_chunked(xh, dt, a, bm, cm):
    bsz, s, g, r, pdim = xh.shape
    n = bm.shape[-1]
    nc = s // CHUNK
    xdt = (xh * dt[..., None]).reshape(bsz, nc, CHUNK, g, r, pdim)
    adt = jnp.moveaxis((dt * a).reshape(bsz, nc, CHUNK, g, r), (3, 4), (1, 2))
    a_cs = jnp.cumsum(adt, axis=-1)
    bc = bm.reshape(bsz, nc, CHUNK, g, n)
    cc = cm.reshape(bsz, nc, CHUNK, g, n)
    seg = a_cs[..., :, None] - a_cs[..., None, :]
    tri = jnp.tril(jnp.ones((CHUNK, CHUNK), dtype=bool))
    lmat = jnp.exp(jnp.where(tri, seg, -jnp.inf))
    y_diag = jnp.einsum("bclgn,bcsgn,bgrcls,bcsgrp->bclgrp", cc, bc, lmat, xdt)
    decay_states = jnp.exp(a_cs[..., -1:] - a_cs)
    states = jnp.einsum("bclgn,bgrcl,bclgrp->bcgrpn", bc, decay_states, xdt)
    chunk_decay = jnp.moveaxis(jnp.exp(a_cs[..., -1]), -1, 0)

    def step(h, inp):
        st, dec = inp
        return h * dec[..., None, None] + st, h

    h0 = jnp.zeros((bsz, g, r, pdim, n), jnp.float32)
    _, prev = lax.scan(step, h0, (jnp.moveaxis(states, 1, 0), chunk_decay))
    prev = jnp.moveaxis(prev, 0, 1)
    y_off = jnp.einsum("bclgn,bcgrpn,bgrcl->bclgrp", cc, prev, jnp.exp(a_cs))
    return (y_diag + y_off).reshape(bsz, s, g, r, pdim)


def ssd_mixer(z, xbc, dt_raw, conv_w, conv_b, dt_bias, a_log, d_skip, norm_w):
    bsz, s = z.shape[:2]
    r = SSD_HEADS // SSD_GROUPS
    xbc = jax.nn.silu(causal_dwconv(xbc, conv_w, conv_b))
    xs, bm, cm = jnp.split(xbc, [D_SSM, D_SSM + SSD_GROUPS * SSD_STATE], axis=-1)
    xh = xs.reshape(bsz, s, SSD_GROUPS, r, SSD_HEADDIM).astype(jnp.float32)
    dt = jax.nn.softplus(dt_raw.astype(jnp.float32) + dt_bias.astype(jnp.float32))
    dt = dt.reshape(bsz, s, SSD_GROUPS, r)
    a = -jnp.exp(a_log.astype(jnp.float32)).reshape(SSD_GROUPS, r)
    y = ssd_chunked(xh, dt, a,
                    bm.reshape(bsz, s, SSD_GROUPS, SSD_STATE).astype(jnp.float32),
                    cm.reshape(bsz, s, SSD_GROUPS, SSD_STATE).astype(jnp.float32))
    y = y + d_skip.astype(jnp.float32).reshape(SSD_GROUPS, r)[..., None] * xh
    y = y.reshape(bsz, s, D_SSM) * jax.nn.silu(z.astype(jnp.float32))
    yg = y.reshape(bsz, s, SSD_GROUPS, D_SSM // SSD_GROUPS)
    yg = yg * lax.rsqrt(jnp.mean(yg * yg, axis=-1, keepdims=True) + NORM_EPS)
    return yg.reshape(bsz, s, D_SSM).astype(z.dtype) * norm_w


def apply_rope(t, cos, sin):
    half = t.shape[-1] // 2
    t1, t2 = t[..., :half], t[..., half:]
    return jnp.concatenate([t1 * cos - t2 * sin, t1 * sin + t2 * cos], axis=-1).astype(t.dtype)


def mla_mixer(q_a, kv_a, cos, sin, q_a_norm_w, w_q_b, kv_a_norm_w, w_kv_b):
    bsz, s = q_a.shape[:2]
    q = (rms_norm(q_a, q_a_norm_w) @ w_q_b).reshape(bsz, s, MLA_HEADS, MLA_NOPE + MLA_ROPE)
    q_nope, q_rope = q[..., :MLA_NOPE], q[..., MLA_NOPE:]
    q_rope = apply_rope(q_rope, cos[:, :, None, :], sin[:, :, None, :])
    c_kv, k_rope = kv_a[..., :MLA_KV_RANK], kv_a[..., MLA_KV_RANK:]
    k_rope = apply_rope(k_rope, cos, sin)
    kv = (rms_norm(c_kv, kv_a_norm_w) @ w_kv_b).reshape(bsz, s, MLA_HEADS, MLA_NOPE + MLA_V)
    k_nope, v = kv[..., :MLA_NOPE], kv[..., MLA_NOPE:]
    scale = 1.0 / math.sqrt(MLA_NOPE + MLA_ROPE)
    nb = s // Q_BLOCK
    k_chunk = jnp.arange(s) // CHUNK

    def attend_block(args):
        qn, qr, start = args
        sc = (jnp.einsum("bqhd,bkhd->bhqk", qn, k_nope)
              + jnp.einsum("bqhr,bkr->bhqk", qr, k_rope)).astype(jnp.float32) * scale
        q_chunk = (start + jnp.arange(Q_BLOCK)) // CHUNK
        mask = k_chunk[None, :] <= q_chunk[:, None]
        sc = jnp.where(mask, sc, -jnp.inf)
        prob = jax.nn.softmax(sc, axis=-1).astype(v.dtype)
        return jnp.einsum("bhqk,bkhd->bqhd", prob, v)

    qn_b = jnp.moveaxis(q_nope.reshape(bsz, nb, Q_BLOCK, MLA_HEADS, MLA_NOPE), 1, 0)
    qr_b = jnp.moveaxis(q_rope.reshape(bsz, nb, Q_BLOCK, MLA_HEADS, MLA_ROPE), 1, 0)
    starts = jnp.arange(nb) * Q_BLOCK
    out = lax.map(attend_block, (qn_b, qr_b, starts))
    return jnp.moveaxis(out, 0, 1).reshape(bsz, s, MLA_HEADS * MLA_V)


def conv_ffn(h, w_up, conv_w, conv_b, w_down):
    u = causal_dwconv(h @ w_up, conv_w, conv_b)
    gate, up = u[..., :D_FF], u[..., D_FF:]
    return (jax.nn.silu(gate) * up) @ w_down


def setup_inputs(seed: int = 0) -> dict:
    key = jax.random.key(seed)
    ks = jax.random.split(key, 32)
    f32 = jnp.float32

    def nrm(k, shape, scale):
        return jax.random.normal(k, shape, f32) * scale

    def gain(k, n):
        return 1.0 + 0.02 * jax.random.normal(k, (DEPTH, n), f32)

    x = jax.random.normal(ks[0], (BATCH, SEQ, D_MODEL), f32)
    p = jax.random.normal(ks[1], (DEPTH, BATCH, SEQ, PLE_DIM), f32)
    offsets = jax.random.randint(ks[2], (BATCH, 1), 0, 64) * CHUNK
    positions = (offsets + jnp.arange(SEQ)[None, :]).astype(jnp.int32)
    dt0 = jnp.exp(jax.random.uniform(ks[3], (DEPTH, SSD_HEADS), f32)
                  * (math.log(0.1) - math.log(0.001)) + math.log(0.001))
    dt_bias = dt0 + jnp.log(-jnp.expm1(-dt0))
    a_log = jnp.log(jax.random.uniform(ks[4], (DEPTH, SSD_HEADS), f32, 1.0, 16.0))
    return {
        "x": x,
        "p": p,
        "positions": positions,
        "mix_norm_w": gain(ks[5], D_MODEL),
        "w_in": nrm(ks[6], (DEPTH, D_MODEL, D_IN_PROJ), D_MODEL ** -0.5),
        "conv_w": nrm(ks[7], (DEPTH, SSD_CONV, SSD_CONV_DIM), SSD_CONV ** -0.5),
        "conv_b": nrm(ks[8], (DEPTH, SSD_CONV_DIM), 0.01),
        "dt_bias": dt_bias,
        "a_log": a_log,
        "d_skip": 1.0 + 0.1 * jax.random.normal(ks[9], (DEPTH, SSD_HEADS), f32),
        "ssd_norm_w": gain(ks[10], D_SSM),
        "q_a_norm_w": gain(ks[11], MLA_Q_RANK),
        "w_q_b": nrm(ks[12], (DEPTH, MLA_Q_RANK, MLA_HEADS * (MLA_NOPE + MLA_ROPE)), MLA_Q_RANK ** -0.5),
        "kv_a_norm_w": gain(ks[13], MLA_KV_RANK),
        "w_kv_b": nrm(ks[14], (DEPTH, MLA_KV_RANK, MLA_HEADS * (MLA_NOPE + MLA_V)), MLA_KV_RANK ** -0.5),
        "w_out": nrm(ks[15], (DEPTH, D_SSM + MLA_HEADS * MLA_V, D_MODEL), D_MODEL ** -0.5),
        "ffn_norm_w": gain(ks[16], D_MODEL),
        "w_ffn_up": nrm(ks[17], (DEPTH, D_MODEL, 2 * D_FF), D_MODEL ** -0.5),
        "ffn_conv_w": nrm(ks[18], (DEPTH, FFN_CONV, 2 * D_FF), FFN_CONV ** -0.5),
        "ffn_conv_b": nrm(ks[19], (DEPTH, 2 * D_FF), 0.01),
        "w_ffn_down": nrm(ks[20], (DEPTH, D_FF, D_MODEL), D_FF ** -0.5),
        "ple_norm_w": gain(ks[21], D_MODEL),
        "w_ple_gate": nrm(ks[22], (DEPTH, D_MODEL, D_MODEL), D_MODEL ** -0.5),
        "b_ple_gate": nrm(ks[23], (DEPTH, D_MODEL), 0.01),
        "w_ple_proj": nrm(ks[24], (DEPTH, PLE_DIM, D_MODEL), PLE_DIM ** -0.5),
        "ple_post_norm_w": gain(ks[25], D_MODEL),
        "final_norm_w": 1.0 + 0.02 * jax.random.normal(ks[26], (D_MODEL,), f32),
    }


def reference(x, p, positions, mix_norm_w, w_in, conv_w, conv_b, dt_bias, a_log, d_skip,
              ssd_norm_w, q_a_norm_w, w_q_b, kv_a_norm_w, w_kv_b, w_out, ffn_norm_w,
              w_ffn_up, ffn_conv_w, ffn_conv_b, w_ffn_down, ple_norm_w, w_ple_gate,
              b_ple_gate, w_ple_proj, ple_post_norm_w, final_norm_w):
    inv_freq = ROPE_THETA ** (-jnp.arange(0, MLA_ROPE, 2, dtype=jnp.float32) / MLA_ROPE)
    ang = positions.astype(jnp.float32)[..., None] * inv_freq
    cos, sin = jnp.cos(ang), jnp.sin(ang)
    splits = [D_SSM, D_SSM + SSD_CONV_DIM, D_SSM + SSD_CONV_DIM + SSD_HEADS,
              D_SSM + SSD_CONV_DIM + SSD_HEADS + MLA_Q_RANK]
    for i in range(DEPTH):
        h = rms_norm(x, mix_norm_w[i])
        z, xbc, dt_raw, q_a, kv_a = jnp.split(h @ w_in[i], splits, axis=-1)
        y_ssd = ssd_mixer(z, xbc, dt_raw, conv_w[i], conv_b[i], dt_bias[i], a_log[i],
                          d_skip[i], ssd_norm_w[i])
        y_mla = mla_mixer(q_a, kv_a, cos, sin, q_a_norm_w[i], w_q_b[i],
                          kv_a_norm_w[i], w_kv_b[i])
        x = x + jnp.concatenate([y_ssd, y_mla], axis=-1) @ w_out[i]
        x = x + conv_ffn(rms_norm(x, ffn_norm_w[i]), w_ffn_up[i], ffn_conv_w[i],
                         ffn_conv_b[i], w_ffn_down[i])
        gate = jax.nn.sigmoid(rms_norm(x, ple_norm_w[i]) @ w_ple_gate[i] + b_ple_gate[i])
        x = x + gate * rms_norm(p[i] @ w_ple_proj[i], ple_post_norm_w[i])
    return rms_norm(x, final_norm_w)
```

```python
import math
import numpy as np
import concourse.bass as bass
import concourse.mybir as mybir
from concourse.bass_utils import run_bass_kernel_spmd

F32 = mybir.dt.float32
BF16 = mybir.dt.bfloat16
I32 = mybir.dt.int32
AF = mybir.ActivationFunctionType
ALU = mybir.AluOpType

D = 2048
S = 2048
NT = 16
DSSM = 1024
NH = 16
HD = 64
NSTATE = 128
CONVD = 1536
QR = 512
KVR = 256
ROPE = 64
NOPE = 128
MH = 8
DFF = 5632
NFB = 44
PLE = 256
EPS = 1e-6
DIN = 3408
DIN_EXT = DIN + 64

DEBUG = {}


class KB:
    EPOCH = 28000
    ALWAYS_INC = True
    NDSEM = 24

    def __init__(self, nc):
        self.nc = nc
        self.q = {"pe": nc.tensor, "act": nc.scalar, "dve": nc.vector, "pool": nc.gpsimd, "sp": nc.sync}
        self.cnt = {e: 0 for e in ("pe", "act", "dve", "pool")}
        self.esems = {e: [] for e in self.cnt}
        self.dsems = [nc.alloc_semaphore(f"dma{i}") for i in range(self.NDSEM)]
        self.dcnt = [0] * self.NDSEM
        self.dlast = [None] * self.NDSEM
        self.dnext = 0
        self.waited = {q: {} for q in self.q}
        self.last_w = {}
        self.readers = {}
        self.pe_pending = []
        self.pool_inflight = []
        self.psems = []
        self.ptoks = []
        self.n_inst = 0

    def _esem(self, e, n):
        j = (n - 1) // self.EPOCH
        while len(self.esems[e]) <= j:
            self.esems[e].append(self.nc.alloc_semaphore(f"s_{e}{len(self.esems[e])}"))
        return self.esems[e][j], n - j * self.EPOCH

    def _wait(self, q, tok):
        if tok is None:
            return
        kind = tok[0]
        if kind == "e":
            _, e, n = tok
            if e == q and q == "pe":
                return
            key = ("e", e)
            if self.waited[q].get(key, 0) >= n:
                return
            if e == q and n > self.cnt[e]:
                return
            sem, val = self._esem(e, n)
            self.q[q].wait_ge(sem, val)
            self.waited[q][key] = n
        else:
            _, i, c = tok
            key = ("d", i)
            if self.waited[q].get(key, 0) >= c:
                return
            self.q[q].wait_ge(self.dsems[i] if i < self.NDSEM else self.psems[i - self.NDSEM], c)
            self.waited[q][key] = c
        self.n_inst += 1

    def _deps(self, q, r, w):
        toks = []
        for x in r:
            t = self.last_w.get(x)
            if t is not None:
                toks.append(t)
        for x in w:
            t = self.last_w.get(x)
            if t is not None:
                toks.append(t)
            for t2 in self.readers.get(x, ()):
                toks.append(t2)
        return toks

    def _register(self, tok, r, w):
        for x in w:
            self.last_w[x] = tok
            self.readers[x] = []
        for x in r:
            lst = self.readers.setdefault(x, [])
            if tok[0] == "e":
                lst[:] = [t for t in lst if not (t[0] == "e" and t[1] == tok[1])]
            lst.append(tok)

    def op(self, q, fn, r=(), w=(), inc=True):
        for t in self._deps(q, r, w):
            self._wait(q, t)
        ins = fn()
        self.n_inst += 1
        if q == "pe" and not inc and not self.ALWAYS_INC:
            self.pe_pending.append((tuple(r), tuple(w)))
            return None
        self.cnt[q] += 1
        n = self.cnt[q]
        sem, _ = self._esem(q, n)
        ins.then_inc(sem, 1)
        tok = ("e", q, n)
        if q == "pe" and self.pe_pending:
            for (pr, pw) in self.pe_pending:
                self._register(tok, pr, pw)
            self.pe_pending = []
        self._register(tok, r, w)
        return tok

    def dma(self, q, out, in_, r=(), w=(), **kw):
        for t in self._deps(q, r, w):
            self._wait(q, t)
        if q == "pool":
            self.pool_inflight.append(None)
            if len(self.pool_inflight) > 2:
                self._wait(q, self.pool_inflight[-3])
        if q == "pool":
            sem = self.nc.alloc_semaphore(f"pdma{len(self.psems)}")
            self.psems.append(sem)
            i = self.NDSEM + len(self.psems) - 1
            ins = self.q[q].dma_start(out=out, in_=in_, **kw)
            self.n_inst += 1
            ins.then_inc(sem, 16)
            tok = ("d", i, 16)
            self.pool_inflight[-1] = tok
            self.ptoks.append(tok)
        else:
            i = self.dnext
            self.dnext = (self.dnext + 1) % self.NDSEM
            self._wait(q, self.dlast[i])
            ins = self.q[q].dma_start(out=out, in_=in_, **kw)
            self.n_inst += 1
            self.dcnt[i] += 16
            ins.then_inc(self.dsems[i], 16)
            tok = ("d", i, self.dcnt[i])
            self.dlast[i] = tok
        self._register(tok, r, w)
        return tok

    def barrier_all(self):
        toks = [("e", e, self.cnt[e]) for e in self.cnt if self.cnt[e] > 0]
        toks += [t for t in self.dlast if t is not None] + list(self.ptoks)
        for q in self.q:
            for t in toks:
                self._wait(q, t)


class Arena:
    def __init__(self, nc, nbytes):
        self.nc = nc
        self.t = nc.alloc_sbuf_tensor("arena", [128, nbytes // 4], F32)
        self.nbytes = nbytes
        self.limit = nbytes
        self.peak = 0
        self.top = 0
        self.marks = []

    def alloc(self, shape, dt, at=None):
        n = int(np.prod(shape))
        esz = 2 if dt == BF16 else 4
        nb = (n * esz + 31) // 32 * 32
        if at is None:
            off = self.top
            assert off + nb <= self.limit, f"SBUF arena overflow {off + nb} > {self.limit}"
            self.top += nb
            self.peak = max(self.peak, self.top)
        else:
            off = at
        a = self.t[:, off // 4:(off + n * esz + 3) // 4]
        if dt != F32:
            a = a.bitcast(dt)
            a = a[:, 0:n]
        if len(shape) == 2:
            a = a.rearrange("p (a b) -> p a b", a=shape[0])
        elif len(shape) == 3:
            a = a.rearrange("p (a b c) -> p a b c", a=shape[0], b=shape[1])
        return a

    def mark(self):
        self.marks.append(self.top)

    def release(self):
        self.top = self.marks.pop()


def build_nc(dbg=None):
    dbg = dbg or {}
    nc = bass.Bass("TRN2", target_bir_lowering=False)
    kb = KB(nc)

    def din(name, shape, dt=F32):
        return nc.dram_tensor(name, list(shape), dt, kind="ExternalInput").ap()

    def dscr(name, shape, dt=BF16):
        return nc.dram_tensor(name, list(shape), dt, kind="Internal").ap()

    x_d = din("x", [S, D])
    p_d = din("p", [S, PLE])
    pos_d = din("pos", [64, S], I32)
    prm_d = din("prm", [128, PRM_N])
    fnw_d = din("fnw_b", [128, D])
    pnw_d = din("pnw_b", [128, D])
    bpg_d = din("bpg", [2, D])
    w_in_d = din("w_in", [D, DIN_EXT])
    w_qb_d = din("w_qb", [QR, MH * 256])
    w_kvb_d = din("w_kvb", [KVR, 2048])
    w_out_d = din("w_out", [D, D])
    w_up_d = din("w_up", [D, 2 * DFF])
    w_dn_d = din("w_dn", [DFF, D])
    w_pg_d = din("w_pg", [D, D])
    w_pp_d = din("w_pp", [PLE, D])
    out_d = nc.dram_tensor("out", [S, D], F32, kind="ExternalOutput").ap()

    wb_in = dscr("wb_in", [D, DIN_EXT])
    wb_qb = dscr("wb_qb", [QR, MH * 256])
    wb_kvb = dscr("wb_kvb", [KVR, 2048])
    wb_out = dscr("wb_out", [D, D])
    wb_up = dscr("wb_up", [D, 2 * DFF])
    wb_dn = dscr("wb_dn", [DFF, D])
    wb_pg = dscr("wb_pg", [D, D])
    wb_pp = dscr("wb_pp", [PLE, D])
    zs_d = dscr("zs_scr", [S, DSSM])

    dbg_out = {}
    for name, sd in dbg.items():
        if name.startswith("_"):
            continue
        shape, dt = sd
        dbg_out[name] = nc.dram_tensor("dbg_" + name, list(shape), dt, kind="ExternalOutput").ap()
    stop_after = dbg.get("_stop")

    ARENA = 207 * 1024
    MIX_OFF = ARENA - 64 * 1024
    ar = Arena(nc, ARENA)
    psum = nc.alloc_psum_tensor("psum", [128, 4096], F32)

    def bank(b):
        return psum[:, b * 512:(b + 1) * 512]

    def PSR(b):
        return ("ps", b)

    V = nc.vector
    A = nc.scalar
    G = nc.gpsimd
    PE = nc.tensor

    def finish():
        kb.barrier_all()
        return nc, kb

    def dump(name, src_ap, regs):
        if name in dbg_out:
            kb.dma("sp", dbg_out[name], src_ap, r=regs)

    def conv_w(dst, src, rows, cols, name, rsplit=1):
        csplit = (cols + 2047) // 2048
        while cols % csplit:
            csplit += 1
        cw_ = cols // csplit
        rh = rows // rsplit
        for ri in range(rsplit):
            d_ = dst[ri * rh:(ri + 1) * rh, :]
            s_ = src[ri * rh:(ri + 1) * rh, :]
            if csplit > 1:
                d_ = d_.rearrange("r (c n) -> r c n", n=cw_)
                s_ = s_.rearrange("r (c n) -> r c n", n=cw_)
            kb.dma("pool", d_, s_, w=[("wsc", name)])

    conv_w(wb_in, w_in_d, D, DIN_EXT, "in", rsplit=2)

    prm = ar.alloc([PRM_N], F32)
    kb.dma("sp", prm, prm_d[:, :], w=["prm"])

    def P_(name):
        o, n = PRM_OFF[name]
        return prm[:, o:o + n]

    ident_f = P_("ident")
    ident_b = ar.alloc([128], BF16)
    kb.op("dve", lambda: V.tensor_copy(out=ident_b, in_=ident_f), r=["prm"], w=["identb"])
    ones_b = ar.alloc([128], BF16)
    kb.op("dve", lambda: V.memset(ones_b, 1.0), w=["onesb"])
    small_t = ar.alloc([8, 8], F32)
    small = [small_t[:, i, :] for i in range(8)]
    junk0 = ar.alloc([2048], BF16)

    conv_w(wb_qb, w_qb_d, QR, MH * 256, "qb")
    conv_w(wb_kvb, w_kvb_d, KVR, 2048, "kvb")
    conv_w(wb_out, w_out_d, D, D, "out", rsplit=2)
    conv_w(wb_up, w_up_d, D, 2 * DFF, "up", rsplit=4)
    conv_w(wb_dn, w_dn_d, DFF, D, "dn", rsplit=4)
    conv_w(wb_pg, w_pg_d, D, D, "pg", rsplit=2)
    conv_w(wb_pp, w_pp_d, PLE, D, "pp")

    B_BASE = [ar.top]
    mixT = ar.alloc([16, S], BF16, at=MIX_OFF)

    cosT = ar.alloc([S], F32)
    sinT = ar.alloc([S], F32)
    qanT = ar.alloc([4, S], BF16)
    ckvT = ar.alloc([2, S], BF16)
    krT = ar.alloc([S], BF16)
    dt_tm = ar.alloc([NT, NH], F32)
    ar.mark()
    xbcT = ar.alloc([12, S], BF16)

    ar.mark()
    posi = ar.alloc([S], I32)
    ang = ar.alloc([S], F32)
    tmpa = ar.alloc([S], F32)
    tmpi = ar.alloc([S], I32)
    kb.dma("sp", posi[0:64, :], pos_d[:, :], w=["posi"])
    kb.op("dve", lambda: V.tensor_copy(out=ang[0:64, :], in_=posi[0:64, :]), r=["posi"], w=["ang"])
    kb.op("dve", lambda: V.tensor_scalar(out=ang[0:64, :], in0=ang[0:64, :], scalar1=P_("invf")[0:64, 0:1], scalar2=None,
                                         op0=ALU.mult), r=["ang", "prm"], w=["ang"])
    TWO_PI = 2.0 * math.pi

    def sin_of(dst, shift, sign_col, tag):
        kb.op("dve", lambda: V.tensor_scalar(out=tmpa[0:64, :], in0=ang[0:64, :], scalar1=shift, scalar2=1.0 / TWO_PI,
                                             op0=ALU.add, op1=ALU.mult), r=["ang"], w=["tmpa"])
        kb.op("dve", lambda: V.tensor_copy(out=tmpi[0:64, :], in_=tmpa[0:64, :]), r=["tmpa"], w=["tmpi"])
        kb.op("dve", lambda: V.tensor_copy(out=tmpa[0:64, :], in_=tmpi[0:64, :]), r=["tmpi"], w=["tmpa"])
        kb.op("dve", lambda: V.scalar_tensor_tensor(out=tmpa[0:64, :], in0=tmpa[0:64, :], scalar=-TWO_PI, in1=ang[0:64, :],
                                                    op0=ALU.mult, op1=ALU.add), r=["tmpa", "ang"], w=["tmpa"])
        kb.op("dve", lambda: V.tensor_scalar(out=tmpa[0:64, :], in0=tmpa[0:64, :], scalar1=shift, scalar2=None,
                                             op0=ALU.add), r=["tmpa"], w=["tmpa"])
        kb.op("dve", lambda: V.tensor_scalar(out=dst[0:64, :], in0=tmpa[0:64, :], scalar1=math.pi, scalar2=-TWO_PI,
                                             op0=ALU.is_gt, op1=ALU.mult), r=["tmpa"], w=[tag])
        kb.op("dve", lambda: V.tensor_tensor(out=tmpa[0:64, :], in0=tmpa[0:64, :], in1=dst[0:64, :], op=ALU.add),
              r=["tmpa", tag], w=["tmpa"])
        kb.op("dve", lambda: V.tensor_scalar(out=dst[0:64, :], in0=tmpa[0:64, :], scalar1=-math.pi, scalar2=TWO_PI,
                                             op0=ALU.is_lt, op1=ALU.mult), r=["tmpa"], w=[tag])
        kb.op("dve", lambda: V.tensor_tensor(out=tmpa[0:64, :], in0=tmpa[0:64, :], in1=dst[0:64, :], op=ALU.add),
              r=["tmpa", tag], w=["tmpa"])
        kb.op("dve", lambda: V.tensor_scalar(out=tmpa[0:64, :], in0=tmpa[0:64, :], scalar1=3.14159, scalar2=-3.14159,
                                             op0=ALU.min, op1=ALU.max), r=["tmpa"], w=["tmpa"])
        kb.op("act", lambda: A.activation(out=dst[0:64, :], in_=tmpa[0:64, :], func=AF.Sin), r=["tmpa"], w=[tag])
        if sign_col is not None:
            kb.op("dve", lambda: V.tensor_scalar(out=dst[0:64, :], in0=dst[0:64, :], scalar1=sign_col, scalar2=None,
                                                 op0=ALU.mult), r=[tag, "prm"], w=[tag])

    sin_of(cosT, math.pi / 2, None, "cosT")
    sin_of(sinT, 0.0, P_("sgn")[0:64, 0:1], "sinT")
    ar.release()
    kb.barrier_all()
    dump("cosT", cosT[0:64, :], ["cosT"])
    dump("sinT", sinT[0:64, :], ["sinT"])
    if stop_after == "C":
        return finish()

    def rstd_of(src, F, rreg):
        ss = small.pop(0)
        small.append(ss)
        k = ("ss", id(ss))
        kb.op("act", lambda: A.activation(out=junk0[:, 0:F], in_=src, func=AF.Square, accum_out=ss[:, 0:1]),
              r=[rreg], w=["junk", k])
        kb.op("dve", lambda: V.tensor_scalar(out=ss[:, 1:2], in0=ss[:, 0:1], scalar1=1.0 / F, scalar2=EPS,
                                             op0=ALU.mult, op1=ALU.add), r=[k], w=[k])
        kb.op("act", lambda: A.sqrt(out=ss[:, 2:3], in_=ss[:, 1:2]), r=[k], w=[k])
        kb.op("dve", lambda: V.reciprocal(out=ss[:, 3:4], in_=ss[:, 2:3]), r=[k], w=[k])
        return ss, k

    def rms_to_T(src_tiles, F, wcol, dstT, tok0, dst_region_fn, psb, xs_bufs):
        nch = F // 128
        nt = len(src_tiles)
        scaled = []
        for i, (src, rreg) in enumerate(src_tiles):
            ss, k = rstd_of(src, F, rreg)
            xs, xsreg = xs_bufs[i]
            kb.op("dve", lambda: V.tensor_scalar(out=xs[:, 0:F], in0=src, scalar1=ss[:, 3:4], scalar2=None, op0=ALU.mult),
                  r=[rreg, k], w=[xsreg])
            scaled.append((xs, xsreg))
        for c in range(nch):
            b = psb[c % len(psb)]
            for i, (xs, xsreg) in enumerate(scaled):
                kb.op("pe", lambda: PE.matmul(bank(b)[:, i * 128:(i + 1) * 128], lhsT=xs[:, c * 128:(c + 1) * 128],
                                              rhs=ident_b, start=True, stop=True),
                      r=[xsreg, "identb"], w=[PSR(b)], inc=(i == nt - 1))
            dst = dstT[:, c, tok0:tok0 + nt * 128]
            if c % 2 == 0:
                kb.op("act", lambda: A.activation(out=dst, in_=bank(b)[:, 0:nt * 128], func=AF.Copy, scale=wcol[:, c:c + 1]),
                      r=[PSR(b), "prm"], w=[dst_region_fn(c)])
            else:
                kb.op("dve", lambda: V.tensor_scalar(out=dst, in0=bank(b)[:, 0:nt * 128], scalar1=wcol[:, c:c + 1], scalar2=None,
                                                     op0=ALU.mult), r=[PSR(b), "prm"], w=[dst_region_fn(c)])

    ar.mark()
    hT = ar.alloc([16, 512], BF16)
    xin = [ar.alloc([D], F32) for _ in range(2)]
    xsb = [ar.alloc([D], BF16) for _ in range(2)]
    xs_bufs = [(xsb[0], ("xsb", 0)), (xsb[1], ("xsb", 1))]
    wtm = [ar.alloc([16, 272], BF16) for _ in range(2)]
    wfm = [ar.alloc([16, 128], BF16) for _ in range(3)]
    stage = [ar.alloc([515], F32) for _ in range(2)]
    acc = [ar.alloc([512], F32) for _ in range(2)]
    halo = ar.alloc([12, 3], F32)
    zst = [ar.alloc([256], BF16) for _ in range(4)]
    qat = [ar.alloc([QR], F32) for _ in range(2)]
    ckt = [ar.alloc([272], F32) for _ in range(2)]
    krst = ar.alloc([2, 512], F32)
    cw = P_("conv_w").rearrange("p (c k) -> p c k", k=4)
    cb = P_("conv_b")
    wtm_i = [0]
    wfm_i = [0]
    zst_i = [0]

    def load_wtm(c0, ncols, extra=None):
        i = wtm_i[0] % 2
        wtm_i[0] += 1
        t = wtm[i]
        kb.dma("sp", t[:, :, 0:ncols], wb_in[:, c0:c0 + ncols].rearrange("(c p) n -> p c n", p=128),
               r=[("wsc", "in")], w=[("wtm", i)])
        if extra is not None:
            e0, en = extra
            kb.dma("sp", t[:, :, ncols:ncols + en], wb_in[:, e0:e0 + en].rearrange("(c p) n -> p c n", p=128),
                   r=[("wsc", "in")], w=[("wtm", i)])
        return t, ("wtm", i)

    def load_wfm(c0, ncols=128):
        i = wfm_i[0] % 3
        wfm_i[0] += 1
        t = wfm[i]
        kb.dma("sp", t[:, :, 0:ncols], wb_in[:, c0:c0 + ncols].rearrange("(c p) n -> p c n", p=128),
               r=[("wsc", "in")], w=[("wfm", i)])
        return t, ("wfm", i)

    for s in range(4):
        t0 = s * 512
        for half in range(2):
            tiles = []
            for i in range(2):
                kb.dma("sp", xin[i], x_d[t0 + half * 256 + i * 128:t0 + half * 256 + (i + 1) * 128, :], w=[("xin", i)])
                tiles.append((xin[i], ("xin", i)))
            rms_to_T(tiles, D, P_("mix_nw"), hT, half * 256, lambda c, half=half: ("hT", c, half), [0, 1], xs_bufs)
        hT_all = [("hT", c, h) for c in range(16) for h in range(2)]

        for c in range(12):
            wt, wreg = load_wfm(DSSM + c * 128)
            b = 2 + (c % 2)
            for kc in range(16):
                kb.op("pe", lambda: PE.matmul(bank(b), lhsT=wt[:, kc, :], rhs=hT[:, kc, :], start=(kc == 0), stop=(kc == 15)),
                      r=[wreg] + hT_all, w=[PSR(b)], inc=(kc == 15))
            st, sreg = stage[c % 2], ("stage", c % 2)
            ac, areg = acc[c % 2], ("acc", c % 2)
            if s == 0:
                kb.op("dve", lambda: V.memset(st[:, 0:3], 0.0), w=[sreg])
            else:
                kb.op("dve", lambda: V.tensor_copy(out=st[:, 0:3], in_=halo[:, c, :]), r=[("halo", c)], w=[sreg])
            kb.op("act", lambda: A.copy(out=st[:, 3:515], in_=bank(b)), r=[PSR(b)], w=[sreg])
            kb.op("dve", lambda: V.tensor_copy(out=halo[:, c, :], in_=st[:, 512:515]), r=[sreg], w=[("halo", c)])
            kb.op("act", lambda: A.activation(out=ac, in_=st[:, 3:515], func=AF.Identity, bias=cb[:, c:c + 1],
                                              scale=cw[:, c, 3:4]), r=[sreg, "prm"], w=[areg])
            for k in range(3):
                kb.op("dve", lambda: V.scalar_tensor_tensor(out=ac, in0=st[:, k:k + 512], scalar=cw[:, c, k:k + 1], in1=ac,
                                                            op0=ALU.mult, op1=ALU.add), r=[sreg, areg, "prm"], w=[areg])
            kb.op("act", lambda: A.activation(out=xbcT[:, c, t0:t0 + 512], in_=ac, func=AF.Silu), r=[areg], w=[("xbcT", c, s)])

        wt, wreg = load_wfm(3344, 128)
        for j in range(2):
            b = 2 + j
            for kc in range(16):
                kb.op("pe", lambda: PE.matmul(bank(b)[0:64, :], lhsT=wt[:, kc, j * 64:(j + 1) * 64], rhs=hT[:, kc, :],
                                              start=(kc == 0), stop=(kc == 15)),
                      r=[wreg] + hT_all, w=[PSR(b)], inc=(kc == 15))
        kb.op("dve", lambda: V.tensor_tensor(out=krst[0:64, 0, :], in0=bank(2)[0:64, :], in1=cosT[0:64, t0:t0 + 512], op=ALU.mult),
              r=[PSR(2), "cosT"], w=["krst0"])
        kb.op("dve", lambda: V.tensor_tensor(out=krst[0:64, 1, :], in0=bank(3)[0:64, :], in1=sinT[0:64, t0:t0 + 512], op=ALU.mult),
              r=[PSR(3), "sinT"], w=["krst1"])
        kb.op("dve", lambda: V.tensor_tensor(out=krT[0:64, t0:t0 + 512], in0=krst[0:64, 0, :], in1=krst[0:64, 1, :], op=ALU.add),
              r=["krst0", "krst1"], w=[("krT", s)])

        for zb_ in range(4):
            wz, rz = load_wtm(zb_ * 256, 256)
            for i in range(4):
                tt = s * 4 + i
                b = 4 + (i % 2)
                for kc in range(16):
                    kb.op("pe", lambda: PE.matmul(bank(b)[:, 0:256], lhsT=hT[:, kc, i * 128:(i + 1) * 128], rhs=wz[:, kc, 0:256],
                                                  start=(kc == 0), stop=(kc == 15)),
                          r=[rz] + hT_all, w=[PSR(b)], inc=(kc == 15))
                zi = zst_i[0] % 4
                zst_i[0] += 1
                kb.op("act", lambda: A.activation(out=zst[zi], in_=bank(b)[:, 0:256], func=AF.Silu), r=[PSR(b)], w=[("zst", zi)])
                kb.dma("sp", zs_d[tt * 128:(tt + 1) * 128, zb_ * 256:(zb_ + 1) * 256], zst[zi], r=[("zst", zi)], w=[("zs_d", tt)])
        wq0, rq0 = load_wtm(2576, 256)
        wq1, rq1 = load_wtm(2576 + 256, 256)
        for i in range(4):
            tt = s * 4 + i
            b = 6
            for j, (wq_, rq_) in enumerate(((wq0, rq0), (wq1, rq1))):
                for kc in range(16):
                    kb.op("pe", lambda: PE.matmul(bank(b)[:, j * 256:(j + 1) * 256], lhsT=hT[:, kc, i * 128:(i + 1) * 128],
                                                  rhs=wq_[:, kc, 0:256], start=(kc == 0), stop=(kc == 15)),
                          r=[rq_] + hT_all, w=[PSR(b)], inc=(kc == 15))
            qa, qreg = qat[i % 2], ("qat", i % 2)
            kb.op("act", lambda: A.copy(out=qa, in_=bank(b)), r=[PSR(b)], w=[qreg])
            rms_to_T([(qa, qreg)], QR, P_("qa_nw"), qanT, tt * 128, lambda c, tt=tt: ("qanT", c, tt), [0, 1], xs_bufs[i % 2:i % 2 + 1])
        wk, rk = load_wtm(3088, 256, extra=(2560, 16))
        for i in range(4):
            tt = s * 4 + i
            b = 7
            for kc in range(16):
                kb.op("pe", lambda: PE.matmul(bank(b)[:, 0:272], lhsT=hT[:, kc, i * 128:(i + 1) * 128], rhs=wk[:, kc, 0:272],
                                              start=(kc == 0), stop=(kc == 15)),
                      r=[rk] + hT_all, w=[PSR(b)], inc=(kc == 15))
            ck, creg = ckt[i % 2], ("ckt", i % 2)
            kb.op("dve", lambda: V.tensor_copy(out=ck, in_=bank(b)[:, 0:272]), r=[PSR(b)], w=[creg])
            kb.op("dve", lambda: V.tensor_tensor(out=dt_tm[:, tt, :], in0=ck[:, 256:272], in1=P_("dt_bias"), op=ALU.add),
                  r=[creg, "prm"], w=[("dt", tt)])
            kb.op("act", lambda: A.activation(out=dt_tm[:, tt, :], in_=dt_tm[:, tt, :], func=AF.Exp), r=[("dt", tt)], w=[("dt", tt)])
            kb.op("act", lambda: A.activation(out=dt_tm[:, tt, :], in_=dt_tm[:, tt, :], func=AF.Ln, bias=1.0), r=[("dt", tt)],
                  w=[("dt", tt)])
            rms_to_T([(ck[:, 0:256], creg)], KVR, P_("kv_nw"), ckvT, tt * 128, lambda c, tt=tt: ("ckvT", c, tt), [0, 1],
                     xs_bufs[i % 2:i % 2 + 1])
    ar.release()
    ar.limit = MIX_OFF

    all_xbc = [("xbcT", c, s) for c in range(12) for s in range(4)]
    dump("xbcT", xbcT, all_xbc)
    dump("qanT", qanT, [("qanT", c, t) for c in range(4) for t in range(NT)])
    dump("ckvT", ckvT, [("ckvT", c, t) for c in range(2) for t in range(NT)])
    dump("krT", krT[0:64, :], [("krT", s) for s in range(4)])
    dump("dt", dt_tm, [("dt", t) for t in range(NT)])
    kb.barrier_all()
    if stop_after == "A1":
        return finish()

    ar.mark()
    R = NH // 2
    negA = ar.alloc([NH], F32)
    kb.op("act", lambda: A.activation(out=negA, in_=P_("a_log"), func=AF.Exp), r=["prm"], w=["negA"])
    kb.op("dve", lambda: V.tensor_scalar(out=negA, in0=negA, scalar1=-1.0, scalar2=None, op0=ALU.mult), r=["negA"], w=["negA"])
    adt = ar.alloc([NT, NH], F32)
    kb.op("dve", lambda: V.tensor_tensor(out=adt, in0=dt_tm, in1=negA.unsqueeze(1).broadcast_to([128, NT, NH]), op=ALU.mult),
          r=[("dt", t) for t in range(NT)] + ["negA"], w=["adt"])
    Tinc = P_("tinc")
    ones_f = P_("ones")
    acs = ar.alloc([NT * NH], F32)
    atot = ar.alloc([NT * NH], F32)
    adt2 = adt.rearrange("p t h -> p (t h)")
    kb.op("pe", lambda: PE.matmul(bank(0)[:, 0:256], lhsT=Tinc, rhs=adt2, start=True, stop=True), r=["adt", "prm"], w=[PSR(0)])
    kb.op("pe", lambda: PE.matmul(bank(1)[:, 0:256], lhsT=ones_f, rhs=adt2, start=True, stop=True), r=["adt", "prm"], w=[PSR(1)])
    kb.op("dve", lambda: V.tensor_copy(out=acs, in_=bank(0)[:, 0:256]), r=[PSR(0)], w=["acs"])
    kb.op("dve", lambda: V.tensor_copy(out=atot, in_=bank(1)[:, 0:256]), r=[PSR(1)], w=["atot"])
    nacs = ar.alloc([NT * NH], F32)
    kb.op("dve", lambda: V.tensor_scalar(out=nacs, in0=acs, scalar1=-1.0, scalar2=None, op0=ALU.mult), r=["acs"], w=["nacs"])
    ea = ar.alloc([NT * NH], F32)
    kb.op("act", lambda: A.activation(out=ea, in_=acs, func=AF.Exp), r=["acs"], w=["ea"])
    dec = ar.alloc([NT * NH], F32)
    kb.op("dve", lambda: V.tensor_tensor(out=dec, in0=atot, in1=acs, op=ALU.subtract), r=["atot", "acs"], w=["dec"])
    kb.op("act", lambda: A.activation(out=dec, in_=dec, func=AF.Exp), r=["dec"], w=["dec"])
    cdec = ar.alloc([NT * NH], F32)
    kb.op("act", lambda: A.activation(out=cdec, in_=atot, func=AF.Exp), r=["atot"], w=["cdec"])

    Hf = [ar.alloc([512], F32) for _ in range(2)]
    Hb = [ar.alloc([512], BF16) for _ in range(2)]
    for g in range(2):
        kb.op("dve", lambda: V.memset(Hf[g], 0.0), w=[("Hf", g)])
        kb.op("dve", lambda: V.memset(Hb[g], 0.0), w=[("Hb", g)])
    x_tm = [ar.alloc([DSSM], BF16) for _ in range(2)]
    xdt = [ar.alloc([DSSM], BF16) for _ in range(2)]
    xdd = [ar.alloc([DSSM], BF16) for _ in range(1)]
    dxs = ar.alloc([DSSM], BF16)
    B_tm = [ar.alloc([256], BF16) for _ in range(2)]
    Gm = [ar.alloc([2, 128], F32) for _ in range(2)]
    Et = [ar.alloc([128], F32) for _ in range(3)]
    Mt = [ar.alloc([128], BF16) for _ in range(3)]
    ysb = [ar.alloc([DSSM], F32) for _ in range(1)]
    zsl = [ar.alloc([DSSM], BF16) for _ in range(1)]
    ynb = [ar.alloc([DSSM], BF16) for _ in range(1)]
    tmpH = ar.alloc([512], F32)
    maskG = P_("tinc")
    dsk = P_("d_skip")
    snw = P_("ssd_nw")
    e_i = [0]
    for c in range(NT):
        t0 = c * 128
        pi = c % 2
        cregs = [("xbcT", cc, c // 4) for cc in range(12)]
        for j in range(8):
            b = 2 + (j // 4)
            kb.op("pe", lambda: PE.matmul(bank(b)[:, (j % 4) * 128:(j % 4 + 1) * 128], lhsT=xbcT[:, j, t0:t0 + 128], rhs=ident_b,
                                          start=True, stop=True), r=cregs + ["identb"], w=[PSR(b)], inc=(j % 4 == 3))
        kb.op("act", lambda: A.copy(out=x_tm[pi][:, 0:512], in_=bank(2)), r=[PSR(2)], w=[("x_tm", pi)])
        kb.op("act", lambda: A.copy(out=x_tm[pi][:, 512:1024], in_=bank(3)), r=[PSR(3)], w=[("x_tm", pi)])
        for g in range(2):
            kb.op("pe", lambda: PE.matmul(bank(4)[:, g * 128:(g + 1) * 128], lhsT=xbcT[:, 8 + g, t0:t0 + 128], rhs=ident_b,
                                          start=True, stop=True), r=cregs + ["identb"], w=[PSR(4)], inc=(g == 1))
        kb.op("dve", lambda: V.tensor_copy(out=B_tm[pi], in_=bank(4)[:, 0:256]), r=[PSR(4)], w=[("B_tm", pi)])
        for g in range(2):
            kb.op("pe", lambda: PE.matmul(bank(5)[:, g * 128:(g + 1) * 128], lhsT=xbcT[:, 8 + g, t0:t0 + 128],
                                          rhs=xbcT[:, 10 + g, t0:t0 + 128], start=True, stop=True),
                  r=cregs, w=[PSR(5)], inc=(g == 1))
        kb.op("dve", lambda: V.tensor_tensor(out=Gm[pi], in0=bank(5)[:, 0:256].rearrange("p (g l) -> p g l", g=2),
                                             in1=maskG.unsqueeze(1).broadcast_to([128, 2, 128]), op=ALU.mult),
              r=[PSR(5), "prm"], w=[("Gm", pi)])
        x3 = x_tm[pi].rearrange("p (h d) -> p h d", h=NH)
        kb.op("pool", lambda: G.tensor_tensor(out=xdt[pi].rearrange("p (h d) -> p h d", h=NH), in0=x3,
                                              in1=dt_tm[:, c, :].unsqueeze(2).broadcast_to([128, NH, HD]), op=ALU.mult),
              r=[("x_tm", pi), ("dt", c)], w=[("xdt", pi)])
        kb.op("pool", lambda: G.tensor_tensor(out=xdd[0].rearrange("p (h d) -> p h d", h=NH),
                                              in0=xdt[pi].rearrange("p (h d) -> p h d", h=NH),
                                              in1=dec[:, c * NH:(c + 1) * NH].unsqueeze(2).broadcast_to([128, NH, HD]), op=ALU.mult),
              r=[("xdt", pi), "dec"], w=["xdd"])
        kb.op("pool", lambda: G.tensor_tensor(out=dxs.rearrange("p (h d) -> p h d", h=NH), in0=x3,
                                              in1=dsk.unsqueeze(2).broadcast_to([128, NH, HD]), op=ALU.mult),
              r=[("x_tm", pi), "prm"], w=["dxs"])
        ys = ysb[0]
        for g in range(2):
            kb.op("pe", lambda: PE.matmul(bank(6 + g), lhsT=xbcT[:, 10 + g, t0:t0 + 128], rhs=Hb[g], start=True, stop=True),
                  r=cregs + [("Hb", g)], w=[PSR(6 + g)])
            kb.op("dve", lambda: V.tensor_tensor(out=ys[:, g * 512:(g + 1) * 512].rearrange("p (h d) -> p h d", h=R),
                                                 in0=bank(6 + g).rearrange("p (h d) -> p h d", h=R),
                                                 in1=ea[:, c * NH + g * R:c * NH + (g + 1) * R].unsqueeze(2).broadcast_to([128, R, HD]),
                                                 op=ALU.mult), r=[PSR(6 + g), "ea"], w=[("ys", g)])
        for g in range(2):
            for hh in range(R):
                h = g * R + hh
                col = c * NH + h
                ei = e_i[0] % 3
                e_i[0] += 1
                bb = h % 2
                kb.op("pe", lambda: PE.matmul(bank(bb)[:, 0:128], lhsT=adt[:, c, h:h + 1].broadcast_to([128, 128]), rhs=Tinc,
                                              start=True, stop=False), r=["adt", "prm"], w=[PSR(bb)], inc=False)
                kb.op("pe", lambda: PE.matmul(bank(bb)[:, 0:128], lhsT=ident_f, rhs=P_("negmask"),
                                              start=False, stop=True), r=["prm"], w=[PSR(bb)])
                kb.op("act", lambda: A.activation(out=Et[ei], in_=bank(bb)[:, 0:128], func=AF.Exp, bias=nacs[:, col:col + 1]),
                      r=[PSR(bb), "nacs"], w=[("Et", ei)])
                kb.op("dve", lambda: V.tensor_tensor(out=Mt[ei], in0=Et[ei], in1=Gm[pi][:, g, :], op=ALU.mult),
                      r=[("Et", ei), ("Gm", pi)], w=[("Mt", ei)])
                kb.op("pe", lambda: PE.matmul(bank(6 + g)[:, hh * 64:(hh + 1) * 64], lhsT=Mt[ei], rhs=xdt[pi][:, h * 64:(h + 1) * 64],
                                              start=True, stop=True), r=[("Mt", ei), ("xdt", pi), ("ys", g)], w=[PSR(6 + g)],
                      inc=(hh == R - 1))
        zl = zsl[0]
        kb.dma("sp", zl, zs_d[t0:t0 + 128, :], r=[("zs_d", c)], w=["zsl"])
        for g in range(2):
            sl = slice(g * 512, (g + 1) * 512)
            kb.op("dve", lambda: V.tensor_tensor(out=ys[:, sl], in0=ys[:, sl], in1=bank(6 + g), op=ALU.add),
                  r=[PSR(6 + g), ("ys", g)], w=[("ys", g)])
        kb.op("dve", lambda: V.tensor_tensor(out=ys, in0=ys, in1=dxs, op=ALU.add),
              r=[("ys", 0), ("ys", 1), "dxs"], w=[("ys", 0), ("ys", 1)])
        kb.op("pool", lambda: G.tensor_tensor(out=ys, in0=ys, in1=zl, op=ALU.mult),
              r=[("ys", 0), ("ys", 1), "zsl"], w=[("ys", 0), ("ys", 1)])
        for g in range(2):
            sl = slice(g * 512, (g + 1) * 512)
            ss, k = rstd_of(ys[:, sl], 512, ("ys", g))
            kb.op("dve", lambda: V.tensor_scalar(out=ynb[0][:, sl], in0=ys[:, sl], scalar1=ss[:, 3:4], scalar2=None, op0=ALU.mult),
                  r=[("ys", g), k], w=[("ynb", g)])
        for j in range(8):
            b = 2 + (j // 4)
            kb.op("pe", lambda: PE.matmul(bank(b)[:, (j % 4) * 128:(j % 4 + 1) * 128], lhsT=ynb[0][:, j * 128:(j + 1) * 128],
                                          rhs=ident_b, start=True, stop=True),
                  r=[("ynb", j // 4), "identb"], w=[PSR(b)], inc=(j % 4 == 3))
        for j in range(8):
            b = 2 + (j // 4)
            kb.op("act", lambda: A.activation(out=mixT[:, j, t0:t0 + 128], in_=bank(b)[:, (j % 4) * 128:(j % 4 + 1) * 128],
                                              func=AF.Copy, scale=snw[:, j:j + 1]), r=[PSR(b), "prm"], w=[("mixT", j, c)])
        if c < NT - 1:
            for g in range(2):
                kb.op("pe", lambda: PE.matmul(bank(4 + g), lhsT=B_tm[pi][:, g * 128:(g + 1) * 128], rhs=xdd[0][:, g * 512:(g + 1) * 512],
                                              start=True, stop=True), r=[("B_tm", pi), "xdd"], w=[PSR(4 + g)])
                kb.op("dve", lambda: V.tensor_tensor(out=tmpH.rearrange("p (h d) -> p h d", h=R),
                                                     in0=Hf[g].rearrange("p (h d) -> p h d", h=R),
                                                     in1=cdec[:, c * NH + g * R:c * NH + (g + 1) * R].unsqueeze(2).broadcast_to([128, R, HD]),
                                                     op=ALU.mult), r=[("Hf", g), "cdec"], w=["tmpH"])
                kb.op("dve", lambda: V.tensor_tensor(out=Hf[g], in0=tmpH, in1=bank(4 + g), op=ALU.add),
                      r=["tmpH", PSR(4 + g)], w=[("Hf", g)])
                kb.op("act", lambda: A.copy(out=Hb[g], in_=Hf[g]), r=[("Hf", g)], w=[("Hb", g)])
    ar.release()
    ar.release()
    dump("mixT_ssd", mixT[:, 0:8, :], [("mixT", j, c) for j in range(8) for c in range(NT)])
    kb.barrier_all()
    if stop_after == "A2":
        return finish()

    ar.mark()
    SCALE = 1.0 / math.sqrt(NOPE + ROPE)
    wq_sb = ar.alloc([4, MH * 256], BF16)
    wkv_sb = ar.alloc([2, 2048], BF16)
    kb.dma("sp", wq_sb, wb_qb.rearrange("(c p) n -> p c n", p=128), r=[("wsc", "qb")], w=["wq_sb"])
    kb.dma("sp", wkv_sb, wb_kvb.rearrange("(c p) n -> p c n", p=128), r=[("wsc", "kvb")], w=["wkv_sb"])
    qnT = [ar.alloc([S], BF16) for _ in range(2)]
    qrT = [ar.alloc([S], BF16) for _ in range(2)]
    knT = [ar.alloc([S], BF16) for _ in range(2)]
    Vt = [ar.alloc([NT, 128], BF16) for _ in range(2)]
    rtmp = ar.alloc([2, 512], F32)
    PT = [ar.alloc([512], BF16) for _ in range(4)]
    rden = [ar.alloc([512], F32) for _ in range(2)]
    all_qan = [("qanT", c, t) for c in range(4) for t in range(NT)]
    all_ckv = [("ckvT", c, t) for c in range(2) for t in range(NT)]
    all_kr = [("krT", s) for s in range(4)]
    pt_i = [0]
    for h in range(MH):
        hp = h % 2
        for tb in range(4):
            ts = slice(tb * 512, (tb + 1) * 512)
            for kc in range(4):
                kb.op("pe", lambda: PE.matmul(bank(0), lhsT=wq_sb[:, kc, h * 256:h * 256 + 128], rhs=qanT[:, kc, ts],
                                              start=(kc == 0), stop=(kc == 3)), r=["wq_sb"] + all_qan, w=[PSR(0)], inc=(kc == 3))
            kb.op("act", lambda: A.copy(out=qnT[hp][:, ts], in_=bank(0)), r=[PSR(0)], w=[("qnT", hp, tb)])
            for j in range(2):
                for kc in range(4):
                    kb.op("pe", lambda: PE.matmul(bank(1 + j)[0:64, :], lhsT=wq_sb[:, kc, h * 256 + 128 + j * 64:h * 256 + 192 + j * 64],
                                                  rhs=qanT[:, kc, ts], start=(kc == 0), stop=(kc == 3)),
                          r=["wq_sb"] + all_qan, w=[PSR(1 + j)], inc=(kc == 3))
            kb.op("dve", lambda: V.tensor_tensor(out=rtmp[0:64, 0, :], in0=bank(1)[0:64, :], in1=cosT[0:64, ts], op=ALU.mult),
                  r=[PSR(1), "cosT"], w=["rtmp0"])
            kb.op("dve", lambda: V.tensor_tensor(out=rtmp[0:64, 1, :], in0=bank(2)[0:64, :], in1=sinT[0:64, ts], op=ALU.mult),
                  r=[PSR(2), "sinT"], w=["rtmp1"])
            kb.op("pool", lambda: G.tensor_tensor(out=qrT[hp][0:64, ts], in0=rtmp[0:64, 0, :], in1=rtmp[0:64, 1, :], op=ALU.add),
                  r=["rtmp0", "rtmp1"], w=[("qrT", hp, tb)])
            for kc in range(2):
                kb.op("pe", lambda: PE.matmul(bank(3), lhsT=wkv_sb[:, kc, h * 256:h * 256 + 128], rhs=ckvT[:, kc, ts],
                                              start=(kc == 0), stop=(kc == 1)), r=["wkv_sb"] + all_ckv, w=[PSR(3)], inc=(kc == 1))
            kb.op("act", lambda: A.copy(out=knT[hp][:, ts], in_=bank(3)), r=[PSR(3)], w=[("knT", hp, tb)])
            for i in range(4):
                tt = tb * 4 + i
                for kc in range(2):
                    kb.op("pe", lambda: PE.matmul(bank(4)[:, i * 128:(i + 1) * 128], lhsT=ckvT[:, kc, tt * 128:(tt + 1) * 128],
                                                  rhs=wkv_sb[:, kc, h * 256 + 128:h * 256 + 256], start=(kc == 0), stop=(kc == 1)),
                          r=["wkv_sb"] + all_ckv, w=[PSR(4)], inc=(i == 3 and kc == 1))
            kb.op("dve", lambda: V.tensor_copy(out=Vt[hp][:, tb * 4:(tb + 1) * 4, :],
                                               in_=bank(4).rearrange("p (i d) -> p i d", i=4)), r=[PSR(4)], w=[("Vt", hp, tb)])
        qn_r = [("qnT", hp, tb) for tb in range(4)]
        qr_r = [("qrT", hp, tb) for tb in range(4)]
        kn_r = [("knT", hp, tb) for tb in range(4)]
        v_r = [("Vt", hp, tb) for tb in range(4)]
        if h == 0:
            dump("qnT0", qnT[0], qn_r)
            dump("qrT0", qrT[0][0:64, :], qr_r)
            dump("knT0", knT[0], kn_r)
        for qb in range(4):
            nk = 4 * qb + 4
            ob, dbk = 5, 6
            for kt in range(nk):
                j = kt - 4 * qb
                c0 = max(j, 0) * 128
                qs = slice(qb * 512 + c0, (qb + 1) * 512)
                ncol = 512 - c0
                sbank = [0, 1, 7][kt % 3]
                kb.op("pe", lambda: PE.matmul(bank(sbank)[:, 0:ncol], lhsT=knT[hp][:, kt * 128:(kt + 1) * 128], rhs=qnT[hp][:, qs],
                                              start=True, stop=False), r=kn_r + qn_r, w=[PSR(sbank)], inc=False)
                kb.op("pe", lambda: PE.matmul(bank(sbank)[:, 0:ncol], lhsT=krT[0:64, kt * 128:(kt + 1) * 128], rhs=qrT[hp][0:64, qs],
                                              start=False, stop=True), r=all_kr + qr_r, w=[PSR(sbank)])
                pi_ = pt_i[0] % 4
                pt_i[0] += 1
                pt = PT[pi_]
                kb.op("act", lambda: A.activation(out=pt[:, 0:ncol], in_=bank(sbank)[:, 0:ncol], func=AF.Exp, scale=SCALE),
                      r=[PSR(sbank)], w=[("PT", pi_)])
                if j >= 0:
                    kb.op("pool", lambda: G.memset(pt[64:128, 0:64], 0.0), r=[("PT", pi_)], w=[("PT", pi_)])
                kb.op("pe", lambda: PE.matmul(bank(ob)[:, c0:512], lhsT=Vt[hp][:, kt, :], rhs=pt[:, 0:ncol],
                                              start=(kt == 0), stop=(kt == nk - 1)), r=v_r + [("PT", pi_)], w=[PSR(ob)], inc=False)
                kb.op("pe", lambda: PE.matmul(bank(dbk)[:, c0:512], lhsT=ones_b, rhs=pt[:, 0:ncol],
                                              start=(kt == 0), stop=(kt == nk - 1)), r=["onesb", ("PT", pi_)], w=[PSR(dbk)])
            rd = rden[qb % 2]
            kb.op("dve", lambda: V.reciprocal(out=rd, in_=bank(dbk)), r=[PSR(dbk)], w=[("rden", qb % 2)])
            kb.op("dve", lambda: V.tensor_tensor(out=mixT[:, 8 + h, qb * 512:(qb + 1) * 512], in0=bank(ob), in1=rd, op=ALU.mult),
                  r=[PSR(ob), ("rden", qb % 2)], w=[("mixT", 8 + h, qb)])
    ar.release()
    mix_all = [("mixT", 8 + h, qb) for h in range(MH) for qb in range(4)] + [("mixT", j, c) for j in range(8) for c in range(NT)]
    dump("mixT", mixT, mix_all)
    kb.barrier_all()
    if stop_after == "A3":
        return finish()

    ar.top = ar.marks[0] if False else ar.top
    ar.top = B_BASE[0]
    xr = [ar.alloc([D], F32) for _ in range(4)]
    hT2 = ar.alloc([16, 512], BF16)
    xs2 = [ar.alloc([D], BF16) for _ in range(2)]
    xs2_bufs = [(xs2[0], ("xs2", 0)), (xs2[1], ("xs2", 1))]
    halo2 = ar.alloc([NFB * 2, 2], F32)
    fcw = P_("ffn_cw").rearrange("p (b k) -> p b k", k=3)
    fcb = P_("ffn_cb")

    for s in range(4):
        t0 = s * 512
        ar.mark()
        wblk = [ar.alloc([16, 512], BF16) for _ in range(2)]
        for i in range(4):
            kb.dma("sp", xr[i], x_d[t0 + i * 128:t0 + (i + 1) * 128, :], w=[("xr", i)])
        mix_r = [("mixT", 8 + h, s) for h in range(MH)] + [("mixT", j, s * 4 + i) for j in range(8) for i in range(4)]
        for cb_ in range(4):
            wt, wreg = wblk[cb_ % 2], ("wblk", cb_ % 2)
            kb.dma("sp", wt, wb_out[:, cb_ * 512:(cb_ + 1) * 512].rearrange("(c p) n -> p c n", p=128),
                   r=[("wsc", "out")], w=[wreg])
            for i in range(4):
                b = (cb_ * 4 + i) % 4
                for kc in range(16):
                    kb.op("pe", lambda: PE.matmul(bank(b), lhsT=mixT[:, kc, t0 + i * 128:t0 + (i + 1) * 128], rhs=wt[:, kc, :],
                                                  start=(kc == 0), stop=(kc == 15)), r=mix_r + [wreg], w=[PSR(b)], inc=(kc == 15))
                cs = slice(cb_ * 512, (cb_ + 1) * 512)
                kb.op("dve", lambda: V.tensor_tensor(out=xr[i][:, cs], in0=xr[i][:, cs], in1=bank(b), op=ALU.add),
                      r=[PSR(b), ("xr", i)], w=[("xr", i)])
        if s == 0:
            for i in range(4):
                dump("x1_%d" % i, xr[i], [("xr", i)])
        ar.release()
        kb.barrier_all()
        if stop_after == "B1":
            return finish()
        for half in range(2):
            rms_to_T([(xr[half * 2], ("xr", half * 2)), (xr[half * 2 + 1], ("xr", half * 2 + 1))], D, P_("ffn_nw"), hT2, half * 256,
                     lambda c, half=half: ("hT2", c, half), [4, 5], xs2_bufs)
        hT2_all = [("hT2", c, hh) for c in range(16) for hh in range(2)]
        ar.mark()
        actT = ar.alloc([22, 512], BF16)
        wup = [ar.alloc([2, 16, 128], BF16) for _ in range(3)]
        wdn = [ar.alloc([2, 1024], BF16) for _ in range(3)]
        stg = [ar.alloc([2, 514], F32) for _ in range(2)]
        accg = [ar.alloc([2, 512], F32) for _ in range(2)]
        wu_i = 0
        wd_i = 0
        for hf in range(2):
            for jb in range(22):
                j = hf * 22 + jb
                wi = wu_i % 3
                wu_i += 1
                wt = wup[wi]
                for g in range(2):
                    kb.dma("sp", wt[:, g, :, :], wb_up[:, g * DFF + j * 128:g * DFF + (j + 1) * 128].rearrange("(c p) n -> p c n", p=128),
                           r=[("wsc", "up")], w=[("wup", wi)])
                pb = [(0, 1), (2, 3)][jb % 2]
                for g in range(2):
                    for kc in range(16):
                        kb.op("pe", lambda: PE.matmul(bank(pb[g]), lhsT=wt[:, g, kc, :], rhs=hT2[:, kc, :], start=(kc == 0), stop=(kc == 15)),
                              r=[("wup", wi)] + hT2_all, w=[PSR(pb[g])], inc=(kc == 15))
                st, sreg = stg[jb % 2], ("stg", jb % 2)
                ac, areg = accg[jb % 2], ("accg", jb % 2)
                hreg = ("halo2", j)
                if s == 0:
                    kb.op("pool", lambda: G.memset(st[:, :, 0:2], 0.0), w=[sreg])
                else:
                    kb.op("pool", lambda: G.tensor_copy(out=st[:, :, 0:2], in_=halo2[:, 2 * j:2 * j + 2, :]), r=[hreg], w=[sreg])
                kb.op("act", lambda: A.copy(out=st[:, 0, 2:514], in_=bank(pb[0])), r=[PSR(pb[0])], w=[sreg])
                kb.op("act", lambda: A.copy(out=st[:, 1, 2:514], in_=bank(pb[1])), r=[PSR(pb[1])], w=[sreg])
                kb.op("pool", lambda: G.tensor_copy(out=halo2[:, 2 * j:2 * j + 2, :], in_=st[:, :, 512:514]), r=[sreg], w=[hreg])
                for g in range(2):
                    jj = g * NFB + j
                    kb.op("act", lambda: A.activation(out=ac[:, g, :], in_=st[:, g, 2:514], func=AF.Identity, bias=fcb[:, jj:jj + 1],
                                                      scale=fcw[:, jj, 2:3]), r=[sreg, "prm"], w=[areg])
                    for k in range(2):
                        eng = "dve"
                        E_ = V
                        kb.op(eng, lambda: E_.scalar_tensor_tensor(out=ac[:, g, :], in0=st[:, g, k:k + 512], scalar=fcw[:, jj, k:k + 1],
                                                                    in1=ac[:, g, :], op0=ALU.mult, op1=ALU.add),
                              r=[sreg, areg, "prm"], w=[areg])
                kb.op("act", lambda: A.activation(out=ac[:, 0, :], in_=ac[:, 0, :], func=AF.Silu), r=[areg], w=[areg])
                kb.op("pool", lambda: G.tensor_tensor(out=actT[:, jb, :], in0=ac[:, 0, :], in1=ac[:, 1, :], op=ALU.mult),
                      r=[areg], w=[("actT", jb)])
            act_all = [("actT", jb) for jb in range(22)]
            for ch in range(2):
                for k2 in range(11):
                    wi = wd_i % 3
                    wd_i += 1
                    wt = wdn[wi]
                    r0 = (hf * 22 + k2 * 2) * 128
                    kb.dma("sp", wt, wb_dn[r0:r0 + 256, ch * 1024:(ch + 1) * 1024].rearrange("(c p) n -> p c n", p=128),
                           r=[("wsc", "dn")], w=[("wdn", wi)])
                    for kk in range(2):
                        jb = k2 * 2 + kk
                        for i in range(4):
                            for cb_ in range(2):
                                b = i * 2 + cb_
                                kb.op("pe", lambda: PE.matmul(bank(b), lhsT=actT[:, jb, i * 128:(i + 1) * 128],
                                                              rhs=wt[:, kk, cb_ * 512:(cb_ + 1) * 512], start=(jb == 0), stop=(jb == 21)),
                                      r=act_all + [("wdn", wi)], w=[PSR(b)], inc=(jb == 21))
                for i in range(4):
                    for cb_ in range(2):
                        b = i * 2 + cb_
                        cs = slice(ch * 1024 + cb_ * 512, ch * 1024 + (cb_ + 1) * 512)
                        kb.op("dve", lambda: V.tensor_tensor(out=xr[i][:, cs], in0=xr[i][:, cs], in1=bank(b), op=ALU.add),
                              r=[PSR(b), ("xr", i)], w=[("xr", i)])
        if s == 0:
            for i in range(4):
                dump("x2_%d" % i, xr[i], [("xr", i)])
        ar.release()
        kb.barrier_all()
        if stop_after == "B2":
            return finish()
        for half in range(2):
            rms_to_T([(xr[half * 2], ("xr", half * 2)), (xr[half * 2 + 1], ("xr", half * 2 + 1))], D, P_("ple_nw"), hT2, half * 256,
                     lambda c, half=half: ("hT2", c, half), [4, 5], xs2_bufs)
        ar.mark()
        wblk = [ar.alloc([16, 512], BF16) for _ in range(2)]
        wpp_sb = ar.alloc([2, D], BF16)
        fnw = ar.alloc([D], F32)
        pnw = ar.alloc([D], F32)
        bhl = ar.alloc([D], BF16)
        pT = ar.alloc([2, 512], BF16)
        pin = [ar.alloc([PLE], F32) for _ in range(2)]
        pinb = [ar.alloc([PLE], BF16) for _ in range(2)]
        tg = [ar.alloc([512], F32) for _ in range(2)]
        tp = [ar.alloc([512], F32) for _ in range(2)]
        kb.dma("sp", wpp_sb, wb_pp.rearrange("(c p) n -> p c n", p=128), r=[("wsc", "pp")], w=["wpp_sb"])
        kb.dma("sp", fnw, fnw_d[:, :], w=["fnw"])
        kb.dma("sp", pnw, pnw_d[:, :], w=["pnw"])
        kb.dma("pool", bhl[0:1, :], bpg_d[0:1, :], w=["bhl"])
        pss = []
        for i in range(4):
            pb_, preg = pin[i % 2], ("pin", i % 2)
            kb.dma("sp", pb_, p_d[t0 + i * 128:t0 + (i + 1) * 128, :], w=[preg])
            kb.op("pool", lambda: G.tensor_copy(out=pinb[i % 2], in_=pb_), r=[preg], w=[("pinb", i % 2)])
            for c in range(2):
                kb.op("pe", lambda: PE.matmul(bank(6)[:, c * 128:(c + 1) * 128], lhsT=pinb[i % 2][:, c * 128:(c + 1) * 128], rhs=ident_b,
                                              start=True, stop=True), r=[("pinb", i % 2), "identb"], w=[PSR(6)], inc=(c == 1))
            kb.op("act", lambda: A.copy(out=pT[:, :, i * 128:(i + 1) * 128], in_=bank(6)[:, 0:256].rearrange("p (c t) -> p c t", c=2)),
                  r=[PSR(6)], w=[("pT", i)])
        for i in range(4):
            ss = small.pop(0)
            small.append(ss)
            k = ("ss", id(ss))
            for cb_ in range(4):
                b = cb_ % 2
                cs = slice(cb_ * 512, (cb_ + 1) * 512)
                for c in range(2):
                    kb.op("pe", lambda: PE.matmul(bank(b), lhsT=pT[:, c, i * 128:(i + 1) * 128], rhs=wpp_sb[:, c, cs],
                                                  start=(c == 0), stop=(c == 1)), r=[("pT", i), "wpp_sb"], w=[PSR(b)], inc=(c == 1))
                kb.op("act", lambda: A.activation(out=junk0[:, 0:512], in_=bank(b), func=AF.Square, accum_out=ss[:, cb_:cb_ + 1]),
                      r=[PSR(b)], w=["junk", k])
            kb.op("dve", lambda: V.tensor_reduce(out=ss[:, 4:5], in_=ss[:, 0:4], axis=mybir.AxisListType.X, op=ALU.add), r=[k], w=[k])
            kb.op("dve", lambda: V.tensor_scalar(out=ss[:, 5:6], in0=ss[:, 4:5], scalar1=1.0 / D, scalar2=EPS, op0=ALU.mult, op1=ALU.add),
                  r=[k], w=[k])
            kb.op("act", lambda: A.sqrt(out=ss[:, 6:7], in_=ss[:, 5:6]), r=[k], w=[k])
            kb.op("dve", lambda: V.reciprocal(out=ss[:, 7:8], in_=ss[:, 6:7]), r=[k], w=[k])
            pss.append((ss, k))
        for cb_ in range(4):
            cs = slice(cb_ * 512, (cb_ + 1) * 512)
            wt, wreg = wblk[cb_ % 2], ("wblk", cb_ % 2)
            kb.dma("sp", wt, wb_pg[:, cs].rearrange("(c p) n -> p c n", p=128), r=[("wsc", "pg")], w=[wreg])
            for i in range(4):
                b = 2 + (i % 2)
                b2 = 4 + (i % 2)
                for kc in range(16):
                    kb.op("pe", lambda: PE.matmul(bank(b), lhsT=hT2[:, kc, i * 128:(i + 1) * 128], rhs=wt[:, kc, :],
                                                  start=(kc == 0), stop=False), r=hT2_all + [wreg], w=[PSR(b)], inc=False)
                kb.op("pe", lambda: PE.matmul(bank(b), lhsT=ones_b[0:1, :], rhs=bhl[0:1, cs], start=False, stop=True),
                      r=["onesb", "bhl"], w=[PSR(b)])
                for c in range(2):
                    kb.op("pe", lambda: PE.matmul(bank(b2), lhsT=pT[:, c, i * 128:(i + 1) * 128], rhs=wpp_sb[:, c, cs],
                                                  start=(c == 0), stop=(c == 1)), r=[("pT", i), "wpp_sb"], w=[PSR(b2)], inc=(c == 1))
                ss, k = pss[i]
                g_, greg = tg[i % 2], ("tg", i % 2)
                p_, pnreg = tp[i % 2], ("tp", i % 2)
                kb.op("act", lambda: A.activation(out=g_, in_=bank(b), func=AF.Sigmoid), r=[PSR(b)], w=[greg])
                kb.op("dve", lambda: V.scalar_tensor_tensor(out=p_, in0=bank(b2), scalar=ss[:, 7:8], in1=pnw[:, cs], op0=ALU.mult, op1=ALU.mult),
                      r=[PSR(b2), k, "pnw"], w=[pnreg])
                kb.op("pool", lambda: G.tensor_tensor(out=p_, in0=p_, in1=g_, op=ALU.mult), r=[pnreg, greg], w=[pnreg])
                kb.op("dve", lambda: V.tensor_tensor(out=xr[i][:, cs], in0=xr[i][:, cs], in1=p_, op=ALU.add), r=[pnreg, ("xr", i)], w=[("xr", i)])
        for i in range(4):
            ss, k = rstd_of(xr[i], D, ("xr", i))
            kb.op("dve", lambda: V.scalar_tensor_tensor(out=xr[i], in0=xr[i], scalar=ss[:, 3:4], in1=fnw, op0=ALU.mult, op1=ALU.mult),
                  r=[("xr", i), k, "fnw"], w=[("xr", i)])
            kb.dma("sp", out_d[t0 + i * 128:t0 + (i + 1) * 128, :], xr[i], r=[("xr", i)], w=[("out", s, i)])
        ar.release()
        kb.barrier_all()
    return finish()


PRM_OFF = {}
PRM_N = 0


def _layout():
    global PRM_N
    off = 0
    for name, n in [("ident", 128), ("tinc", 128), ("ones", 128), ("negmask", 128), ("invf", 1), ("sgn", 1),
                    ("mix_nw", 16), ("ffn_nw", 16), ("ple_nw", 16), ("qa_nw", 4), ("kv_nw", 2), ("ssd_nw", 8),
                    ("conv_w", 48), ("conv_b", 12), ("dt_bias", 16), ("a_log", 16), ("d_skip", 16),
                    ("ffn_cw", 264), ("ffn_cb", 88)]:
        PRM_OFF[name] = (off, n)
        off += n
    PRM_N = off


_layout()


def _pack_params(inp):
    prm = np.zeros((128, PRM_N), np.float32)

    def put(name, arr):
        o, n = PRM_OFF[name]
        prm[:, o:o + n] = np.asarray(arr, np.float32).reshape(128, n)

    def fm(v):
        v = np.asarray(v, np.float32).reshape(-1)
        return v.reshape(-1, 128).T

    put("ident", np.eye(128, dtype=np.float32))
    put("tinc", np.triu(np.ones((128, 128), np.float32)))
    put("ones", np.ones((128, 128), np.float32))
    put("negmask", -30000.0 * np.tril(np.ones((128, 128), np.float32), -1))
    invf = (10000.0 ** (-np.arange(0, ROPE, 2, dtype=np.float32) / ROPE)).astype(np.float32)
    put("invf", np.tile(invf, 4).reshape(128, 1))
    sgn = np.ones((128, 1), np.float32)
    sgn[0:32] = -1.0
    sgn[64:96] = -1.0
    put("sgn", sgn)
    put("mix_nw", fm(inp["mix_norm_w"]))
    put("ffn_nw", fm(inp["ffn_norm_w"]))
    put("ple_nw", fm(inp["ple_norm_w"]))
    put("qa_nw", fm(inp["q_a_norm_w"]))
    put("kv_nw", fm(inp["kv_a_norm_w"]))
    put("ssd_nw", fm(inp["ssd_norm_w"]))
    cwt = np.asarray(inp["conv_w"], np.float32).reshape(4, 12, 128).transpose(2, 1, 0)
    put("conv_w", cwt.reshape(128, 48))
    put("conv_b", fm(inp["conv_b"]))
    for nm, key in (("dt_bias", "dt_bias"), ("a_log", "a_log"), ("d_skip", "d_skip")):
        put(nm, np.broadcast_to(np.asarray(inp[key], np.float32).reshape(1, 16), (128, 16)))
    fcw = np.asarray(inp["ffn_conv_w"], np.float32).reshape(3, 88, 128).transpose(2, 1, 0)
    put("ffn_cw", fcw.reshape(128, 264))
    put("ffn_cb", fm(inp["ffn_conv_b"]))
    return prm


def _weights(inp):
    w_in = np.asarray(inp["w_in"], np.float32).reshape(D, DIN)
    w_in_ext = np.ascontiguousarray(np.concatenate([w_in, w_in[:, 3376:3408], w_in[:, 3344:3376]], axis=1))
    wq = np.asarray(inp["w_q_b"], np.float32).reshape(QR, MH, 192)
    wq_ext = np.ascontiguousarray(np.concatenate([wq, wq[:, :, 160:192], wq[:, :, 128:160]], axis=2).reshape(QR, MH * 256))
    return {
        "w_in": w_in_ext,
        "w_qb": wq_ext,
        "w_kvb": np.ascontiguousarray(np.asarray(inp["w_kv_b"], np.float32).reshape(KVR, 2048)),
        "w_out": np.ascontiguousarray(np.asarray(inp["w_out"], np.float32).reshape(D, D)),
        "w_up": np.ascontiguousarray(np.asarray(inp["w_ffn_up"], np.float32).reshape(D, 2 * DFF)),
        "w_dn": np.ascontiguousarray(np.asarray(inp["w_ffn_down"], np.float32).reshape(DFF, D)),
        "w_pg": np.ascontiguousarray(np.asarray(inp["w_ple_gate"], np.float32).reshape(D, D)),
        "w_pp": np.ascontiguousarray(np.asarray(inp["w_ple_proj"], np.float32).reshape(PLE, D)),
    }


def make_in_maps(inp, cores):
    prm = _pack_params(inp)
    wts = _weights(inp)
    x = np.asarray(inp["x"], np.float32)
    p = np.asarray(inp["p"], np.float32)
    pos = np.asarray(inp["positions"], np.int32)
    fnw_b = np.ascontiguousarray(np.broadcast_to(np.asarray(inp["final_norm_w"], np.float32).reshape(1, D), (128, D)))
    pnw_b = np.ascontiguousarray(np.broadcast_to(np.asarray(inp["ple_post_norm_w"], np.float32).reshape(1, D), (128, D)))
    bpg = np.ascontiguousarray(np.broadcast_to(np.asarray(inp["b_ple_gate"], np.float32).reshape(1, D), (2, D)))
    maps = []
    for b in cores:
        m = {"x": np.ascontiguousarray(x[b]), "p": np.ascontiguousarray(p[0, b]),
             "pos": np.ascontiguousarray(np.broadcast_to(pos[b].reshape(1, S), (64, S))), "prm": prm,
             "fnw_b": fnw_b, "pnw_b": pnw_b, "bpg": bpg}
        m.update(wts)
        maps.append(m)
    return maps


def kernel(**inputs):
    nc, _ = build_nc()
    in_maps = make_in_maps(inputs, list(range(8)))
    res = run_bass_kernel_spmd(nc, in_maps, core_ids=list(range(8)))
    return np.stack([np.asarray(r["out"], np.float32) for r in res.results], axis=0)
```

```python
import math
import numpy as np
import concourse.bass as bass
import concourse.mybir as mybir
from concourse.bass_utils import run_bass_kernel_spmd

F32 = mybir.dt.float32
BF16 = mybir.dt.bfloat16
I32 = mybir.dt.int32
AF = mybir.ActivationFunctionType
ALU = mybir.AluOpType

D = 2048
S = 2048
NT = 16
DSSM = 1024
NH = 16
HD = 64
NSTATE = 128
CONVD = 1536
QR = 512
KVR = 256
ROPE = 64
NOPE = 128
MH = 8
DFF = 5632
NFB = 44
PLE = 256
EPS = 1e-6
DIN = 3408
DIN_EXT = DIN + 64

DEBUG = {}


class KB:
    EPOCH = 28000
    ALWAYS_INC = True
    NDSEM = 24

    def __init__(self, nc):
        self.nc = nc
        self.q = {"pe": nc.tensor, "act": nc.scalar, "dve": nc.vector, "pool": nc.gpsimd, "sp": nc.sync}
        self.cnt = {e: 0 for e in ("pe", "act", "dve", "pool")}
        self.esems = {e: [] for e in self.cnt}
        self.dsems = [nc.alloc_semaphore(f"dma{i}") for i in range(self.NDSEM)]
        self.dcnt = [0] * self.NDSEM
        self.dlast = [None] * self.NDSEM
        self.dnext = 0
        self.waited = {q: {} for q in self.q}
        self.last_w = {}
        self.readers = {}
        self.pe_pending = []
        self.pool_inflight = []
        self.psems = []
        self.ptoks = []
        self.n_inst = 0

    def _esem(self, e, n):
        j = (n - 1) // self.EPOCH
        while len(self.esems[e]) <= j:
            self.esems[e].append(self.nc.alloc_semaphore(f"s_{e}{len(self.esems[e])}"))
        return self.esems[e][j], n - j * self.EPOCH

    def _wait(self, q, tok):
        if tok is None:
            return
        kind = tok[0]
        if kind == "e":
            _, e, n = tok
            if e == q and q == "pe":
                return
            key = ("e", e)
            if self.waited[q].get(key, 0) >= n:
                return
            if e == q and n > self.cnt[e]:
                return
            sem, val = self._esem(e, n)
            self.q[q].wait_ge(sem, val)
            self.waited[q][key] = n
        else:
            _, i, c = tok
            key = ("d", i)
            if self.waited[q].get(key, 0) >= c:
                return
            self.q[q].wait_ge(self.dsems[i] if i < self.NDSEM else self.psems[i - self.NDSEM], c)
            self.waited[q][key] = c
        self.n_inst += 1

    def _deps(self, q, r, w):
        toks = []
        for x in r:
            t = self.last_w.get(x)
            if t is not None:
                toks.append(t)
        for x in w:
            t = self.last_w.get(x)
            if t is not None:
                toks.append(t)
            for t2 in self.readers.get(x, ()):
                toks.append(t2)
        return toks

    def _register(self, tok, r, w):
        for x in w:
            self.last_w[x] = tok
            self.readers[x] = []
        for x in r:
            lst = self.readers.setdefault(x, [])
            if tok[0] == "e":
                lst[:] = [t for t in lst if not (t[0] == "e" and t[1] == tok[1])]
            lst.append(tok)

    def op(self, q, fn, r=(), w=(), inc=True):
        for t in self._deps(q, r, w):
            self._wait(q, t)
        ins = fn()
        self.n_inst += 1
        if q == "pe" and not inc and not self.ALWAYS_INC:
            self.pe_pending.append((tuple(r), tuple(w)))
            return None
        self.cnt[q] += 1
        n = self.cnt[q]
        sem, _ = self._esem(q, n)
        ins.then_inc(sem, 1)
        tok = ("e", q, n)
        if q == "pe" and self.pe_pending:
            for (pr, pw) in self.pe_pending:
                self._register(tok, pr, pw)
            self.pe_pending = []
        self._register(tok, r, w)
        return tok

    def dma(self, q, out, in_, r=(), w=(), nobarrier=False, **kw):
        kw_nobarrier = nobarrier
        for t in self._deps(q, r, w):
            self._wait(q, t)
        if q == "pool":
            self.pool_inflight.append(None)
            if len(self.pool_inflight) > 2:
                self._wait(q, self.pool_inflight[-3])
        if q == "pool":
            sem = self.nc.alloc_semaphore(f"pdma{len(self.psems)}")
            self.psems.append(sem)
            i = self.NDSEM + len(self.psems) - 1
            ins = self.q[q].dma_start(out=out, in_=in_, **kw)
            self.n_inst += 1
            ins.then_inc(sem, 16)
            tok = ("d", i, 16)
            self.pool_inflight[-1] = tok
            if not kw_nobarrier:
                self.ptoks.append(tok)
        else:
            i = self.dnext
            self.dnext = (self.dnext + 1) % self.NDSEM
            self._wait(q, self.dlast[i])
            ins = self.q[q].dma_start(out=out, in_=in_, **kw)
            self.n_inst += 1
            self.dcnt[i] += 16
            ins.then_inc(self.dsems[i], 16)
            tok = ("d", i, self.dcnt[i])
            self.dlast[i] = tok
        self._register(tok, r, w)
        return tok

    def barrier_all(self):
        toks = [("e", e, self.cnt[e]) for e in self.cnt if self.cnt[e] > 0]
        toks += [t for t in self.dlast if t is not None] + list(self.ptoks)
        for q in self.q:
            for t in toks:
                self._wait(q, t)


class Arena:
    def __init__(self, nc, nbytes):
        self.nc = nc
        self.t = nc.alloc_sbuf_tensor("arena", [128, nbytes // 4], F32)
        self.nbytes = nbytes
        self.limit = nbytes
        self.peak = 0
        self.log = []
        self.top = 0
        self.marks = []

    def alloc(self, shape, dt, at=None):
        n = int(np.prod(shape))
        esz = 2 if dt == BF16 else 4
        nb = (n * esz + 31) // 32 * 32
        if at is None:
            off = self.top
            assert off + nb <= self.limit, f"SBUF arena overflow {off + nb} > {self.limit}"
            self.top += nb
            self.peak = max(self.peak, self.top)
        else:
            off = at
        a = self.t[:, off // 4:(off + n * esz + 3) // 4]
        if dt != F32:
            a = a.bitcast(dt)
            a = a[:, 0:n]
        if len(shape) == 2:
            a = a.rearrange("p (a b) -> p a b", a=shape[0])
        elif len(shape) == 3:
            a = a.rearrange("p (a b c) -> p a b c", a=shape[0], b=shape[1])
        return a

    def mark(self):
        self.marks.append(self.top)

    def release(self):
        self.log.append(self.top)
        self.top = self.marks.pop()


def build_nc(dbg=None):
    dbg = dbg or {}
    nc = bass.Bass("TRN2", target_bir_lowering=False)
    kb = KB(nc)

    def din(name, shape, dt=F32):
        return nc.dram_tensor(name, list(shape), dt, kind="ExternalInput").ap()

    def dscr(name, shape, dt=BF16):
        return nc.dram_tensor(name, list(shape), dt, kind="Internal").ap()

    x_d = din("x", [S, D])
    p_d = din("p", [S, PLE])
    pos_d = din("pos", [64, S], I32)
    prm_d = din("prm", [128, PRM_N])
    fnw_d = din("fnw_b", [128, D])
    pnw_d = din("pnw_b", [128, D])
    bpg_d = din("bpg", [2, D])
    w_in_d = din("w_in", [D, DIN_EXT])
    w_qb_d = din("w_qb", [QR, MH * 256])
    w_kvb_d = din("w_kvb", [KVR, 2048])
    w_out_d = din("w_out", [D, D])
    w_up_d = din("w_up", [D, 2 * DFF])
    w_dn_d = din("w_dn", [DFF, D])
    w_pg_d = din("w_pg", [D, D])
    w_pp_d = din("w_pp", [PLE, D])
    out_d = nc.dram_tensor("out", [S, D], F32, kind="ExternalOutput").ap()

    wb_in = dscr("wb_in", [D, DIN_EXT])
    wb_qb = dscr("wb_qb", [QR, MH * 256])
    wb_kvb = dscr("wb_kvb", [KVR, 2048])
    wb_out = dscr("wb_out", [D, D])
    wb_up = dscr("wb_up", [D, 2 * DFF])
    wb_dn = dscr("wb_dn", [DFF, D])
    wb_pg = dscr("wb_pg", [D, D])
    wb_pp = dscr("wb_pp", [PLE, D])
    zs_d = dscr("zs_scr", [S, DSSM])

    dbg_out = {}
    for name, sd in dbg.items():
        if name.startswith("_"):
            continue
        shape, dt = sd
        dbg_out[name] = nc.dram_tensor("dbg_" + name, list(shape), dt, kind="ExternalOutput").ap()
    stop_after = dbg.get("_stop")

    ARENA = 207 * 1024
    MIX_OFF = ARENA - 64 * 1024
    ar = Arena(nc, ARENA)
    psum = nc.alloc_psum_tensor("psum", [128, 4096], F32)

    def bank(b):
        return psum[:, b * 512:(b + 1) * 512]

    def PSR(b):
        return ("ps", b)

    V = nc.vector
    A = nc.scalar
    G = nc.gpsimd
    PE = nc.tensor

    def finish():
        kb.barrier_all()
        kb.arena_log = ar.log
        return nc, kb

    def dump(name, src_ap, regs):
        if name in dbg_out:
            kb.dma("sp", dbg_out[name], src_ap, r=regs)

    def conv_w(dst, src, rows, cols, name, rsplit=1):
        csplit = (cols + 2047) // 2048
        while cols % csplit:
            csplit += 1
        cw_ = cols // csplit
        rh = rows // rsplit
        for ri in range(rsplit):
            d_ = dst[ri * rh:(ri + 1) * rh, :]
            s_ = src[ri * rh:(ri + 1) * rh, :]
            if csplit > 1:
                d_ = d_.rearrange("r (c n) -> r c n", n=cw_)
                s_ = s_.rearrange("r (c n) -> r c n", n=cw_)
            kb.dma("pool", d_, s_, w=[("wsc", name)])

    conv_w(wb_in, w_in_d, D, DIN_EXT, "in", rsplit=2)

    prm = ar.alloc([PRM_N], F32)
    kb.dma("sp", prm, prm_d[:, :], w=["prm"])

    def P_(name):
        o, n = PRM_OFF[name]
        return prm[:, o:o + n]

    ident_f = P_("ident")
    ident_b = ar.alloc([128], BF16)
    kb.op("dve", lambda: V.tensor_copy(out=ident_b, in_=ident_f), r=["prm"], w=["identb"])
    ones_b = ar.alloc([128], BF16)
    kb.op("dve", lambda: V.memset(ones_b, 1.0), w=["onesb"])
    small_t = ar.alloc([8, 8], F32)
    small = [small_t[:, i, :] for i in range(8)]
    junk0 = ar.alloc([2048], BF16)

    conv_w(wb_qb, w_qb_d, QR, MH * 256, "qb")
    conv_w(wb_kvb, w_kvb_d, KVR, 2048, "kvb")
    conv_w(wb_out, w_out_d, D, D, "out", rsplit=2)
    conv_w(wb_up, w_up_d, D, 2 * DFF, "up", rsplit=4)
    conv_w(wb_dn, w_dn_d, DFF, D, "dn", rsplit=4)
    conv_w(wb_pg, w_pg_d, D, D, "pg", rsplit=2)
    conv_w(wb_pp, w_pp_d, PLE, D, "pp")

    B_BASE = [ar.top]
    mixT = ar.alloc([16, S], BF16, at=MIX_OFF)

    cosT = ar.alloc([S], F32)
    sinT = ar.alloc([S], F32)
    qanT = ar.alloc([4, S], BF16)
    ckvT = ar.alloc([2, S], BF16)
    krT = ar.alloc([S], BF16)
    dt_tm = ar.alloc([NT, NH], F32)
    ar.mark()
    xbcT = ar.alloc([12, S], BF16)

    ar.mark()
    posi = ar.alloc([S], I32)
    ang = ar.alloc([S], F32)
    tmpa = ar.alloc([S], F32)
    tmpi = ar.alloc([S], I32)
    kb.dma("sp", posi[0:64, :], pos_d[:, :], w=["posi"])
    kb.op("dve", lambda: V.tensor_copy(out=ang[0:64, :], in_=posi[0:64, :]), r=["posi"], w=["ang"])
    kb.op("dve", lambda: V.tensor_scalar(out=ang[0:64, :], in0=ang[0:64, :], scalar1=P_("invf")[0:64, 0:1], scalar2=None,
                                         op0=ALU.mult), r=["ang", "prm"], w=["ang"])
    TWO_PI = 2.0 * math.pi

    def sin_of(dst, shift, sign_col, tag):
        kb.op("dve", lambda: V.tensor_scalar(out=tmpa[0:64, :], in0=ang[0:64, :], scalar1=shift, scalar2=1.0 / TWO_PI,
                                             op0=ALU.add, op1=ALU.mult), r=["ang"], w=["tmpa"])
        kb.op("dve", lambda: V.tensor_copy(out=tmpi[0:64, :], in_=tmpa[0:64, :]), r=["tmpa"], w=["tmpi"])
        kb.op("dve", lambda: V.tensor_copy(out=tmpa[0:64, :], in_=tmpi[0:64, :]), r=["tmpi"], w=["tmpa"])
        kb.op("dve", lambda: V.scalar_tensor_tensor(out=tmpa[0:64, :], in0=tmpa[0:64, :], scalar=-TWO_PI, in1=ang[0:64, :],
                                                    op0=ALU.mult, op1=ALU.add), r=["tmpa", "ang"], w=["tmpa"])
        kb.op("dve", lambda: V.tensor_scalar(out=tmpa[0:64, :], in0=tmpa[0:64, :], scalar1=shift, scalar2=None,
                                             op0=ALU.add), r=["tmpa"], w=["tmpa"])
        kb.op("dve", lambda: V.tensor_scalar(out=dst[0:64, :], in0=tmpa[0:64, :], scalar1=math.pi, scalar2=-TWO_PI,
                                             op0=ALU.is_gt, op1=ALU.mult), r=["tmpa"], w=[tag])
        kb.op("dve", lambda: V.tensor_tensor(out=tmpa[0:64, :], in0=tmpa[0:64, :], in1=dst[0:64, :], op=ALU.add),
              r=["tmpa", tag], w=["tmpa"])
        kb.op("dve", lambda: V.tensor_scalar(out=dst[0:64, :], in0=tmpa[0:64, :], scalar1=-math.pi, scalar2=TWO_PI,
                                             op0=ALU.is_lt, op1=ALU.mult), r=["tmpa"], w=[tag])
        kb.op("dve", lambda: V.tensor_tensor(out=tmpa[0:64, :], in0=tmpa[0:64, :], in1=dst[0:64, :], op=ALU.add),
              r=["tmpa", tag], w=["tmpa"])
        kb.op("dve", lambda: V.tensor_scalar(out=tmpa[0:64, :], in0=tmpa[0:64, :], scalar1=3.14159, scalar2=-3.14159,
                                             op0=ALU.min, op1=ALU.max), r=["tmpa"], w=["tmpa"])
        kb.op("act", lambda: A.activation(out=dst[0:64, :], in_=tmpa[0:64, :], func=AF.Sin), r=["tmpa"], w=[tag])
        if sign_col is not None:
            kb.op("dve", lambda: V.tensor_scalar(out=dst[0:64, :], in0=dst[0:64, :], scalar1=sign_col, scalar2=None,
                                                 op0=ALU.mult), r=[tag, "prm"], w=[tag])

    sin_of(cosT, math.pi / 2, None, "cosT")
    sin_of(sinT, 0.0, P_("sgn")[0:64, 0:1], "sinT")
    ar.release()
    kb.barrier_all()
    dump("cosT", cosT[0:64, :], ["cosT"])
    dump("sinT", sinT[0:64, :], ["sinT"])
    if stop_after == "C":
        return finish()

    def rstd_of(src, F, rreg):
        ss = small.pop(0)
        small.append(ss)
        k = ("ss", id(ss))
        kb.op("act", lambda: A.activation(out=junk0[:, 0:F], in_=src, func=AF.Square, accum_out=ss[:, 0:1]),
              r=[rreg], w=["junk", k])
        kb.op("dve", lambda: V.tensor_scalar(out=ss[:, 1:2], in0=ss[:, 0:1], scalar1=1.0 / F, scalar2=EPS,
                                             op0=ALU.mult, op1=ALU.add), r=[k], w=[k])
        kb.op("act", lambda: A.sqrt(out=ss[:, 2:3], in_=ss[:, 1:2]), r=[k], w=[k])
        kb.op("dve", lambda: V.reciprocal(out=ss[:, 3:4], in_=ss[:, 2:3]), r=[k], w=[k])
        return ss, k

    def rms_to_T(src_tiles, F, wcol, dstT, tok0, dst_region_fn, psb, xs_bufs):
        nch = F // 128
        nt = len(src_tiles)
        scaled = []
        for i, (src, rreg) in enumerate(src_tiles):
            ss, k = rstd_of(src, F, rreg)
            xs, xsreg = xs_bufs[i]
            kb.op("dve", lambda: V.tensor_scalar(out=xs[:, 0:F], in0=src, scalar1=ss[:, 3:4], scalar2=None, op0=ALU.mult),
                  r=[rreg, k], w=[xsreg])
            scaled.append((xs, xsreg))
        for c in range(nch):
            b = psb[c % len(psb)]
            for i, (xs, xsreg) in enumerate(scaled):
                kb.op("pe", lambda: PE.matmul(bank(b)[:, i * 128:(i + 1) * 128], lhsT=xs[:, c * 128:(c + 1) * 128],
                                              rhs=ident_b, start=True, stop=True),
                      r=[xsreg, "identb"], w=[PSR(b)], inc=(i == nt - 1))
            dst = dstT[:, c, tok0:tok0 + nt * 128]
            if c % 2 == 0:
                kb.op("act", lambda: A.activation(out=dst, in_=bank(b)[:, 0:nt * 128], func=AF.Copy, scale=wcol[:, c:c + 1]),
                      r=[PSR(b), "prm"], w=[dst_region_fn(c)])
            else:
                kb.op("dve", lambda: V.tensor_scalar(out=dst, in0=bank(b)[:, 0:nt * 128], scalar1=wcol[:, c:c + 1], scalar2=None,
                                                     op0=ALU.mult), r=[PSR(b), "prm"], w=[dst_region_fn(c)])

    ar.mark()
    hT = ar.alloc([16, 512], BF16)
    xin = [ar.alloc([D], F32) for _ in range(2)]
    xsb = [ar.alloc([D], BF16) for _ in range(2)]
    xs_bufs = [(xsb[0], ("xsb", 0)), (xsb[1], ("xsb", 1))]
    wtm = [ar.alloc([16, 512], BF16) for _ in range(2)]
    wfm = [ar.alloc([16, 128], BF16) for _ in range(3)]
    stage = [ar.alloc([515], F32) for _ in range(2)]
    acc = [ar.alloc([512], F32) for _ in range(2)]
    halo = ar.alloc([12, 3], F32)
    zst = [ar.alloc([512], BF16) for _ in range(2)]
    qat = [ar.alloc([QR], F32) for _ in range(2)]
    ckt = [ar.alloc([272], F32) for _ in range(2)]
    krst = ar.alloc([2, 512], F32)
    cw = P_("conv_w").rearrange("p (c k) -> p c k", k=4)
    cb = P_("conv_b")
    wtm_i = [0]
    wfm_i = [0]
    zst_i = [0]

    def load_wtm(c0, ncols, extra=None):
        i = wtm_i[0] % 2
        wtm_i[0] += 1
        t = wtm[i]
        kb.dma("sp", t[:, :, 0:ncols], wb_in[:, c0:c0 + ncols].rearrange("(c p) n -> p c n", p=128),
               r=[("wsc", "in")], w=[("wtm", i)])
        if extra is not None:
            e0, en = extra
            kb.dma("sp", t[:, :, ncols:ncols + en], wb_in[:, e0:e0 + en].rearrange("(c p) n -> p c n", p=128),
                   r=[("wsc", "in")], w=[("wtm", i)])
        return t, ("wtm", i)

    def load_wfm(c0, ncols=128):
        i = wfm_i[0] % 3
        wfm_i[0] += 1
        t = wfm[i]
        kb.dma("sp", t[:, :, 0:ncols], wb_in[:, c0:c0 + ncols].rearrange("(c p) n -> p c n", p=128),
               r=[("wsc", "in")], w=[("wfm", i)])
        return t, ("wfm", i)

    for s in range(4):
        t0 = s * 512
        for half in range(2):
            tiles = []
            for i in range(2):
                kb.dma("sp", xin[i], x_d[t0 + half * 256 + i * 128:t0 + half * 256 + (i + 1) * 128, :], w=[("xin", i)])
                tiles.append((xin[i], ("xin", i)))
            rms_to_T(tiles, D, P_("mix_nw"), hT, half * 256, lambda c, half=half: ("hT", c, half), [0, 1], xs_bufs)
        hT_all = [("hT", c, h) for c in range(16) for h in range(2)]

        for c in range(12):
            wt, wreg = load_wfm(DSSM + c * 128)
            b = 2 + (c % 2)
            for kc in range(16):
                kb.op("pe", lambda: PE.matmul(bank(b), lhsT=wt[:, kc, :], rhs=hT[:, kc, :], start=(kc == 0), stop=(kc == 15)),
                      r=[wreg] + hT_all, w=[PSR(b)], inc=(kc == 15))
            st, sreg = stage[c % 2], ("stage", c % 2)
            ac, areg = acc[c % 2], ("acc", c % 2)
            if s == 0:
                kb.op("dve", lambda: V.memset(st[:, 0:3], 0.0), w=[sreg])
            else:
                kb.op("dve", lambda: V.tensor_copy(out=st[:, 0:3], in_=halo[:, c, :]), r=[("halo", c)], w=[sreg])
            kb.op("act", lambda: A.copy(out=st[:, 3:515], in_=bank(b)), r=[PSR(b)], w=[sreg])
            kb.op("dve", lambda: V.tensor_copy(out=halo[:, c, :], in_=st[:, 512:515]), r=[sreg], w=[("halo", c)])
            kb.op("act", lambda: A.activation(out=ac, in_=st[:, 3:515], func=AF.Identity, bias=cb[:, c:c + 1],
                                              scale=cw[:, c, 3:4]), r=[sreg, "prm"], w=[areg])
            for k in range(3):
                kb.op("dve", lambda: V.scalar_tensor_tensor(out=ac, in0=st[:, k:k + 512], scalar=cw[:, c, k:k + 1], in1=ac,
                                                            op0=ALU.mult, op1=ALU.add), r=[sreg, areg, "prm"], w=[areg])
            kb.op("act", lambda: A.activation(out=xbcT[:, c, t0:t0 + 512], in_=ac, func=AF.Silu), r=[areg], w=[("xbcT", c, s)])

        wt, wreg = load_wfm(3344, 128)
        for j in range(2):
            b = 2 + j
            for kc in range(16):
                kb.op("pe", lambda: PE.matmul(bank(b)[0:64, :], lhsT=wt[:, kc, j * 64:(j + 1) * 64], rhs=hT[:, kc, :],
                                              start=(kc == 0), stop=(kc == 15)),
                      r=[wreg] + hT_all, w=[PSR(b)], inc=(kc == 15))
        kb.op("dve", lambda: V.tensor_tensor(out=krst[0:64, 0, :], in0=bank(2)[0:64, :], in1=cosT[0:64, t0:t0 + 512], op=ALU.mult),
              r=[PSR(2), "cosT"], w=["krst0"])
        kb.op("dve", lambda: V.tensor_tensor(out=krst[0:64, 1, :], in0=bank(3)[0:64, :], in1=sinT[0:64, t0:t0 + 512], op=ALU.mult),
              r=[PSR(3), "sinT"], w=["krst1"])
        kb.op("dve", lambda: V.tensor_tensor(out=krT[0:64, t0:t0 + 512], in0=krst[0:64, 0, :], in1=krst[0:64, 1, :], op=ALU.add),
              r=["krst0", "krst1"], w=[("krT", s)])

        for zb_ in range(2):
            wz, rz = load_wtm(zb_ * 512, 512)
            for i in range(4):
                tt = s * 4 + i
                b = 4 + (i % 2)
                for kc in range(16):
                    kb.op("pe", lambda: PE.matmul(bank(b), lhsT=hT[:, kc, i * 128:(i + 1) * 128], rhs=wz[:, kc, :],
                                                  start=(kc == 0), stop=(kc == 15)),
                          r=[rz] + hT_all, w=[PSR(b)], inc=(kc == 15))
                zi = zst_i[0] % 2
                zst_i[0] += 1
                kb.op("act", lambda: A.activation(out=zst[zi], in_=bank(b), func=AF.Silu), r=[PSR(b)], w=[("zst", zi)])
                kb.dma("sp", zs_d[tt * 128:(tt + 1) * 128, zb_ * 512:(zb_ + 1) * 512], zst[zi], r=[("zst", zi)], w=[("zs_d", tt)])
        wq_, rq_ = load_wtm(2576, 512)
        wk, rk = load_wtm(3088, 256, extra=(2560, 16))
        jobs = [("q", i) for i in range(4)] + [("k", i) for i in range(4)]

        def emit_mm(job, n):
            kind, i = job
            b = 6 + (n % 2)
            if kind == "q":
                for kc in range(16):
                    kb.op("pe", lambda: PE.matmul(bank(b), lhsT=hT[:, kc, i * 128:(i + 1) * 128], rhs=wq_[:, kc, :],
                                                  start=(kc == 0), stop=(kc == 15)),
                          r=[rq_] + hT_all, w=[PSR(b)], inc=(kc == 15))
                qa, qreg = qat[i % 2], ("qat", i % 2)
                kb.op("act", lambda: A.copy(out=qa, in_=bank(b)), r=[PSR(b)], w=[qreg])
            else:
                for kc in range(16):
                    kb.op("pe", lambda: PE.matmul(bank(b)[:, 0:272], lhsT=hT[:, kc, i * 128:(i + 1) * 128], rhs=wk[:, kc, 0:272],
                                                  start=(kc == 0), stop=(kc == 15)),
                          r=[rk] + hT_all, w=[PSR(b)], inc=(kc == 15))
                ck, creg = ckt[i % 2], ("ckt", i % 2)
                kb.op("dve", lambda: V.tensor_copy(out=ck, in_=bank(b)[:, 0:272]), r=[PSR(b)], w=[creg])

        def emit_post(job, n):
            kind, i = job
            tt = s * 4 + i
            if kind == "q":
                qa, qreg = qat[i % 2], ("qat", i % 2)
                rms_to_T([(qa, qreg)], QR, P_("qa_nw"), qanT, tt * 128, lambda c, tt=tt: ("qanT", c, tt), [0, 1],
                         xs_bufs[n % 2:n % 2 + 1])
            else:
                ck, creg = ckt[i % 2], ("ckt", i % 2)
                kb.op("dve", lambda: V.tensor_tensor(out=dt_tm[:, tt, :], in0=ck[:, 256:272], in1=P_("dt_bias"), op=ALU.add),
                      r=[creg, "prm"], w=[("dt", tt)])
                kb.op("act", lambda: A.activation(out=dt_tm[:, tt, :], in_=dt_tm[:, tt, :], func=AF.Exp), r=[("dt", tt)], w=[("dt", tt)])
                kb.op("act", lambda: A.activation(out=dt_tm[:, tt, :], in_=dt_tm[:, tt, :], func=AF.Ln, bias=1.0), r=[("dt", tt)],
                      w=[("dt", tt)])
                rms_to_T([(ck[:, 0:256], creg)], KVR, P_("kv_nw"), ckvT, tt * 128, lambda c, tt=tt: ("ckvT", c, tt), [0, 1],
                         xs_bufs[n % 2:n % 2 + 1])

        emit_mm(jobs[0], 0)
        for n, job in enumerate(jobs):
            if n + 1 < len(jobs):
                emit_mm(jobs[n + 1], n + 1)
            emit_post(job, n)
    ar.release()
    ar.limit = MIX_OFF

    all_xbc = [("xbcT", c, s) for c in range(12) for s in range(4)]
    dump("xbcT", xbcT, all_xbc)
    dump("qanT", qanT, [("qanT", c, t) for c in range(4) for t in range(NT)])
    dump("ckvT", ckvT, [("ckvT", c, t) for c in range(2) for t in range(NT)])
    dump("krT", krT[0:64, :], [("krT", s) for s in range(4)])
    dump("dt", dt_tm, [("dt", t) for t in range(NT)])
    kb.barrier_all()
    if stop_after == "A1":
        return finish()

    ar.mark()
    R = NH // 2
    negA = ar.alloc([NH], F32)
    kb.op("act", lambda: A.activation(out=negA, in_=P_("a_log"), func=AF.Exp), r=["prm"], w=["negA"])
    kb.op("dve", lambda: V.tensor_scalar(out=negA, in0=negA, scalar1=-1.0, scalar2=None, op0=ALU.mult), r=["negA"], w=["negA"])
    adt = ar.alloc([NT, NH], F32)
    kb.op("dve", lambda: V.tensor_tensor(out=adt, in0=dt_tm, in1=negA.unsqueeze(1).broadcast_to([128, NT, NH]), op=ALU.mult),
          r=[("dt", t) for t in range(NT)] + ["negA"], w=["adt"])
    Tinc = P_("tinc")
    ones_f = P_("ones")
    acs = ar.alloc([NT * NH], F32)
    atot = ar.alloc([NT * NH], F32)
    adt2 = adt.rearrange("p t h -> p (t h)")
    kb.op("pe", lambda: PE.matmul(bank(0)[:, 0:256], lhsT=Tinc, rhs=adt2, start=True, stop=True), r=["adt", "prm"], w=[PSR(0)])
    kb.op("pe", lambda: PE.matmul(bank(1)[:, 0:256], lhsT=ones_f, rhs=adt2, start=True, stop=True), r=["adt", "prm"], w=[PSR(1)])
    kb.op("dve", lambda: V.tensor_copy(out=acs, in_=bank(0)[:, 0:256]), r=[PSR(0)], w=["acs"])
    kb.op("dve", lambda: V.tensor_copy(out=atot, in_=bank(1)[:, 0:256]), r=[PSR(1)], w=["atot"])
    nacs = ar.alloc([NT * NH], F32)
    kb.op("dve", lambda: V.tensor_scalar(out=nacs, in0=acs, scalar1=-1.0, scalar2=None, op0=ALU.mult), r=["acs"], w=["nacs"])
    ea = ar.alloc([NT * NH], F32)
    kb.op("act", lambda: A.activation(out=ea, in_=acs, func=AF.Exp), r=["acs"], w=["ea"])
    dec = ar.alloc([NT * NH], F32)
    kb.op("dve", lambda: V.tensor_tensor(out=dec, in0=atot, in1=acs, op=ALU.subtract), r=["atot", "acs"], w=["dec"])
    kb.op("act", lambda: A.activation(out=dec, in_=dec, func=AF.Exp), r=["dec"], w=["dec"])
    cdec = ar.alloc([NT * NH], F32)
    kb.op("act", lambda: A.activation(out=cdec, in_=atot, func=AF.Exp), r=["atot"], w=["cdec"])

    Hf = [ar.alloc([512], F32) for _ in range(2)]
    Hb = [ar.alloc([512], BF16) for _ in range(2)]
    for g in range(2):
        kb.op("dve", lambda: V.memset(Hf[g], 0.0), w=[("Hf", g)])
        kb.op("dve", lambda: V.memset(Hb[g], 0.0), w=[("Hb", g)])
    x_tm = [ar.alloc([DSSM], BF16) for _ in range(2)]
    xdt = [ar.alloc([DSSM], BF16) for _ in range(2)]
    xdd = [ar.alloc([DSSM], BF16) for _ in range(1)]
    dxs = ar.alloc([DSSM], BF16)
    B_tm = [ar.alloc([256], BF16) for _ in range(2)]
    Gm = [ar.alloc([2, 128], F32) for _ in range(2)]
    Et = [ar.alloc([128], F32) for _ in range(4)]
    Mt = [ar.alloc([128], BF16) for _ in range(4)]
    ysb = [ar.alloc([DSSM], F32) for _ in range(1)]
    zsl = [ar.alloc([DSSM], BF16) for _ in range(1)]
    ynb = [ar.alloc([DSSM], BF16) for _ in range(1)]
    tmpH = ar.alloc([512], F32)
    maskG = P_("tinc")
    dsk = P_("d_skip")
    snw = P_("ssd_nw")
    e_i = [0]
    for c in range(NT):
        t0 = c * 128
        pi = c % 2
        cregs = [("xbcT", cc, c // 4) for cc in range(12)]
        for j in range(8):
            b = 2 + (j // 4)
            kb.op("pe", lambda: PE.matmul(bank(b)[:, (j % 4) * 128:(j % 4 + 1) * 128], lhsT=xbcT[:, j, t0:t0 + 128], rhs=ident_b,
                                          start=True, stop=True), r=cregs + ["identb"], w=[PSR(b)], inc=(j % 4 == 3))
        kb.op("act", lambda: A.copy(out=x_tm[pi][:, 0:512], in_=bank(2)), r=[PSR(2)], w=[("x_tm", pi)])
        kb.op("act", lambda: A.copy(out=x_tm[pi][:, 512:1024], in_=bank(3)), r=[PSR(3)], w=[("x_tm", pi)])
        for g in range(2):
            kb.op("pe", lambda: PE.matmul(bank(4)[:, g * 128:(g + 1) * 128], lhsT=xbcT[:, 8 + g, t0:t0 + 128], rhs=ident_b,
                                          start=True, stop=True), r=cregs + ["identb"], w=[PSR(4)], inc=(g == 1))
        kb.op("dve", lambda: V.tensor_copy(out=B_tm[pi], in_=bank(4)[:, 0:256]), r=[PSR(4)], w=[("B_tm", pi)])
        for g in range(2):
            kb.op("pe", lambda: PE.matmul(bank(4)[:, 256 + g * 128:256 + (g + 1) * 128], lhsT=xbcT[:, 8 + g, t0:t0 + 128],
                                          rhs=xbcT[:, 10 + g, t0:t0 + 128], start=True, stop=True),
                  r=cregs, w=[PSR(4)], inc=(g == 1))
        kb.op("dve", lambda: V.tensor_tensor(out=Gm[pi], in0=bank(4)[:, 256:512].rearrange("p (g l) -> p g l", g=2),
                                             in1=maskG.unsqueeze(1).broadcast_to([128, 2, 128]), op=ALU.mult),
              r=[PSR(4), "prm"], w=[("Gm", pi)])
        x3 = x_tm[pi].rearrange("p (h d) -> p h d", h=NH)
        kb.op("pool", lambda: G.tensor_tensor(out=xdt[pi].rearrange("p (h d) -> p h d", h=NH), in0=x3,
                                              in1=dt_tm[:, c, :].unsqueeze(2).broadcast_to([128, NH, HD]), op=ALU.mult),
              r=[("x_tm", pi), ("dt", c)], w=[("xdt", pi)])
        kb.op("pool", lambda: G.tensor_tensor(out=xdd[0].rearrange("p (h d) -> p h d", h=NH),
                                              in0=xdt[pi].rearrange("p (h d) -> p h d", h=NH),
                                              in1=dec[:, c * NH:(c + 1) * NH].unsqueeze(2).broadcast_to([128, NH, HD]), op=ALU.mult),
              r=[("xdt", pi), "dec"], w=["xdd"])
        kb.op("pool", lambda: G.tensor_tensor(out=dxs.rearrange("p (h d) -> p h d", h=NH), in0=x3,
                                              in1=dsk.unsqueeze(2).broadcast_to([128, NH, HD]), op=ALU.mult),
              r=[("x_tm", pi), "prm"], w=["dxs"])
        ys = ysb[0]
        for g in range(2):
            kb.op("pe", lambda: PE.matmul(bank(6 + g), lhsT=xbcT[:, 10 + g, t0:t0 + 128], rhs=Hb[g], start=True, stop=True),
                  r=cregs + [("Hb", g)], w=[PSR(6 + g)])
            kb.op("dve", lambda: V.tensor_tensor(out=ys[:, g * 512:(g + 1) * 512].rearrange("p (h d) -> p h d", h=R),
                                                 in0=bank(6 + g).rearrange("p (h d) -> p h d", h=R),
                                                 in1=ea[:, c * NH + g * R:c * NH + (g + 1) * R].unsqueeze(2).broadcast_to([128, R, HD]),
                                                 op=ALU.mult), r=[PSR(6 + g), "ea"], w=[("ys", g)])
        DEPTH = 3

        def emit_seg(idx):
            h = idx
            rr = [0, 1, 5][(c * NH + idx) % 3]
            dst = bank(rr)[:, 0:128]
            reg = PSR(rr)
            kb.op("pe", lambda: PE.matmul(dst, lhsT=adt[:, c, h:h + 1].broadcast_to([128, 128]), rhs=Tinc,
                                          start=True, stop=False), r=["adt", "prm"], w=[reg], inc=False)
            kb.op("pe", lambda: PE.matmul(dst, lhsT=ident_f, rhs=P_("negmask"), start=False, stop=True), r=["prm"], w=[reg])
            return dst, reg

        segq = [emit_seg(i) for i in range(DEPTH)]
        for idx in range(NH):
            g, hh = idx // R, idx % R
            h = idx
            col = c * NH + h
            ei = e_i[0] % 4
            e_i[0] += 1
            dst, reg = segq.pop(0)
            kb.op("act", lambda: A.activation(out=Et[ei], in_=dst, func=AF.Exp, bias=nacs[:, col:col + 1]),
                  r=[reg, "nacs"], w=[("Et", ei)])
            kb.op("dve", lambda: V.tensor_tensor(out=Mt[ei], in0=Et[ei], in1=Gm[pi][:, g, :], op=ALU.mult),
                  r=[("Et", ei), ("Gm", pi)], w=[("Mt", ei)])
            kb.op("pe", lambda: PE.matmul(bank(6 + g)[:, hh * 64:(hh + 1) * 64], lhsT=Mt[ei], rhs=xdt[pi][:, h * 64:(h + 1) * 64],
                                          start=True, stop=True), r=[("Mt", ei), ("xdt", pi), ("ys", g)], w=[PSR(6 + g)],
                  inc=(hh == R - 1))
            if idx + DEPTH < NH:
                segq.append(emit_seg(idx + DEPTH))
        zl = zsl[0]
        kb.dma("sp", zl, zs_d[t0:t0 + 128, :], r=[("zs_d", c)], w=["zsl"])
        for g in range(2):
            sl = slice(g * 512, (g + 1) * 512)
            kb.op("dve", lambda: V.tensor_tensor(out=ys[:, sl], in0=ys[:, sl], in1=bank(6 + g), op=ALU.add),
                  r=[PSR(6 + g), ("ys", g)], w=[("ys", g)])
        kb.op("dve", lambda: V.tensor_tensor(out=ys, in0=ys, in1=dxs, op=ALU.add),
              r=[("ys", 0), ("ys", 1), "dxs"], w=[("ys", 0), ("ys", 1)])
        kb.op("pool", lambda: G.tensor_tensor(out=ys, in0=ys, in1=zl, op=ALU.mult),
              r=[("ys", 0), ("ys", 1), "zsl"], w=[("ys", 0), ("ys", 1)])
        for g in range(2):
            sl = slice(g * 512, (g + 1) * 512)
            ss, k = rstd_of(ys[:, sl], 512, ("ys", g))
            kb.op("dve", lambda: V.tensor_scalar(out=ynb[0][:, sl], in0=ys[:, sl], scalar1=ss[:, 3:4], scalar2=None, op0=ALU.mult),
                  r=[("ys", g), k], w=[("ynb", g)])
        for j in range(8):
            b = 2 + (j // 4)
            kb.op("pe", lambda: PE.matmul(bank(b)[:, (j % 4) * 128:(j % 4 + 1) * 128], lhsT=ynb[0][:, j * 128:(j + 1) * 128],
                                          rhs=ident_b, start=True, stop=True),
                  r=[("ynb", j // 4), "identb"], w=[PSR(b)], inc=(j % 4 == 3))
        for j in range(8):
            b = 2 + (j // 4)
            kb.op("act", lambda: A.activation(out=mixT[:, j, t0:t0 + 128], in_=bank(b)[:, (j % 4) * 128:(j % 4 + 1) * 128],
                                              func=AF.Copy, scale=snw[:, j:j + 1]), r=[PSR(b), "prm"], w=[("mixT", j, c)])
        if c < NT - 1:
            for g in range(2):
                kb.op("pe", lambda: PE.matmul(bank(4), lhsT=B_tm[pi][:, g * 128:(g + 1) * 128], rhs=xdd[0][:, g * 512:(g + 1) * 512],
                                              start=True, stop=True), r=[("B_tm", pi), "xdd"], w=[PSR(4)])
                kb.op("dve", lambda: V.tensor_tensor(out=tmpH.rearrange("p (h d) -> p h d", h=R),
                                                     in0=Hf[g].rearrange("p (h d) -> p h d", h=R),
                                                     in1=cdec[:, c * NH + g * R:c * NH + (g + 1) * R].unsqueeze(2).broadcast_to([128, R, HD]),
                                                     op=ALU.mult), r=[("Hf", g), "cdec"], w=["tmpH"])
                kb.op("dve", lambda: V.tensor_tensor(out=Hf[g], in0=tmpH, in1=bank(4), op=ALU.add),
                      r=["tmpH", PSR(4)], w=[("Hf", g)])
                kb.op("act", lambda: A.copy(out=Hb[g], in_=Hf[g]), r=[("Hf", g)], w=[("Hb", g)])
    ar.release()
    ar.release()
    dump("mixT_ssd", mixT[:, 0:8, :], [("mixT", j, c) for j in range(8) for c in range(NT)])
    kb.barrier_all()
    if stop_after == "A2":
        return finish()

    ar.mark()
    SCALE = 1.0 / math.sqrt(NOPE + ROPE)
    wq_sb = ar.alloc([4, MH * 256], BF16)
    wkv_sb = ar.alloc([2, 2048], BF16)
    kb.dma("sp", wq_sb, wb_qb.rearrange("(c p) n -> p c n", p=128), r=[("wsc", "qb")], w=["wq_sb"])
    kb.dma("sp", wkv_sb, wb_kvb.rearrange("(c p) n -> p c n", p=128), r=[("wsc", "kvb")], w=["wkv_sb"])
    qnT = [ar.alloc([S], BF16) for _ in range(2)]
    qrT = [ar.alloc([S], BF16) for _ in range(2)]
    knT = [ar.alloc([S], BF16) for _ in range(2)]
    Vt = [ar.alloc([NT, 128], BF16) for _ in range(2)]
    rtmp = ar.alloc([2, 512], F32)
    PT = [ar.alloc([512], BF16) for _ in range(4)]
    rden = [ar.alloc([512], F32) for _ in range(2)]
    all_qan = [("qanT", c, t) for c in range(4) for t in range(NT)]
    all_ckv = [("ckvT", c, t) for c in range(2) for t in range(NT)]
    all_kr = [("krT", s) for s in range(4)]
    pt_i = [0]
    sc_i = [0]
    for h in range(MH):
        hp = h % 2
        for tb in range(4):
            ts = slice(tb * 512, (tb + 1) * 512)
            for kc in range(4):
                kb.op("pe", lambda: PE.matmul(bank(0), lhsT=wq_sb[:, kc, h * 256:h * 256 + 128], rhs=qanT[:, kc, ts],
                                              start=(kc == 0), stop=(kc == 3)), r=["wq_sb"] + all_qan, w=[PSR(0)], inc=(kc == 3))
            kb.op("act", lambda: A.copy(out=qnT[hp][:, ts], in_=bank(0)), r=[PSR(0)], w=[("qnT", hp, tb)])
            for j in range(2):
                for kc in range(4):
                    kb.op("pe", lambda: PE.matmul(bank(1 + j)[0:64, :], lhsT=wq_sb[:, kc, h * 256 + 128 + j * 64:h * 256 + 192 + j * 64],
                                                  rhs=qanT[:, kc, ts], start=(kc == 0), stop=(kc == 3)),
                          r=["wq_sb"] + all_qan, w=[PSR(1 + j)], inc=(kc == 3))
            kb.op("dve", lambda: V.tensor_tensor(out=rtmp[0:64, 0, :], in0=bank(1)[0:64, :], in1=cosT[0:64, ts], op=ALU.mult),
                  r=[PSR(1), "cosT"], w=["rtmp0"])
            kb.op("dve", lambda: V.tensor_tensor(out=rtmp[0:64, 1, :], in0=bank(2)[0:64, :], in1=sinT[0:64, ts], op=ALU.mult),
                  r=[PSR(2), "sinT"], w=["rtmp1"])
            kb.op("pool", lambda: G.tensor_tensor(out=qrT[hp][0:64, ts], in0=rtmp[0:64, 0, :], in1=rtmp[0:64, 1, :], op=ALU.add),
                  r=["rtmp0", "rtmp1"], w=[("qrT", hp, tb)])
            for kc in range(2):
                kb.op("pe", lambda: PE.matmul(bank(3), lhsT=wkv_sb[:, kc, h * 256:h * 256 + 128], rhs=ckvT[:, kc, ts],
                                              start=(kc == 0), stop=(kc == 1)), r=["wkv_sb"] + all_ckv, w=[PSR(3)], inc=(kc == 1))
            kb.op("act", lambda: A.copy(out=knT[hp][:, ts], in_=bank(3)), r=[PSR(3)], w=[("knT", hp, tb)])
            for i in range(4):
                tt = tb * 4 + i
                for kc in range(2):
                    kb.op("pe", lambda: PE.matmul(bank(4)[:, i * 128:(i + 1) * 128], lhsT=ckvT[:, kc, tt * 128:(tt + 1) * 128],
                                                  rhs=wkv_sb[:, kc, h * 256 + 128:h * 256 + 256], start=(kc == 0), stop=(kc == 1)),
                          r=["wkv_sb"] + all_ckv, w=[PSR(4)], inc=(i == 3 and kc == 1))
            kb.op("dve", lambda: V.tensor_copy(out=Vt[hp][:, tb * 4:(tb + 1) * 4, :],
                                               in_=bank(4).rearrange("p (i d) -> p i d", i=4)), r=[PSR(4)], w=[("Vt", hp, tb)])
        qn_r = [("qnT", hp, tb) for tb in range(4)]
        qr_r = [("qrT", hp, tb) for tb in range(4)]
        kn_r = [("knT", hp, tb) for tb in range(4)]
        v_r = [("Vt", hp, tb) for tb in range(4)]
        if h == 0:
            dump("qnT0", qnT[0], qn_r)
            dump("qrT0", qrT[0][0:64, :], qr_r)
            dump("knT0", knT[0], kn_r)
        for qb in range(4):
            nk = 4 * qb + 4
            ob, dbk = 5, 6
            def emit_qk(kt):
                j = kt - 4 * qb
                c0 = max(j, 0) * 128
                qs = slice(qb * 512 + c0, (qb + 1) * 512)
                ncol = 512 - c0
                sbank = [0, 1, 7][sc_i[0] % 3]
                sc_i[0] += 1
                kb.op("pe", lambda: PE.matmul(bank(sbank)[:, 0:ncol], lhsT=knT[hp][:, kt * 128:(kt + 1) * 128], rhs=qnT[hp][:, qs],
                                              start=True, stop=False), r=kn_r + qn_r, w=[PSR(sbank)], inc=False)
                kb.op("pe", lambda: PE.matmul(bank(sbank)[:, 0:ncol], lhsT=krT[0:64, kt * 128:(kt + 1) * 128], rhs=qrT[hp][0:64, qs],
                                              start=False, stop=True), r=all_kr + qr_r, w=[PSR(sbank)])
                return sbank

            qkq = [emit_qk(kt) for kt in range(min(2, nk))]
            for kt in range(nk):
                j = kt - 4 * qb
                c0 = max(j, 0) * 128
                ncol = 512 - c0
                sbank = qkq.pop(0)
                pi_ = pt_i[0] % 4
                pt_i[0] += 1
                pt = PT[pi_]
                kb.op("act", lambda: A.activation(out=pt[:, 0:ncol], in_=bank(sbank)[:, 0:ncol], func=AF.Exp, scale=SCALE),
                      r=[PSR(sbank)], w=[("PT", pi_)])
                if j >= 0:
                    kb.op("pool", lambda: G.memset(pt[64:128, 0:64], 0.0), r=[("PT", pi_)], w=[("PT", pi_)])
                if kt + 2 < nk:
                    qkq.append(emit_qk(kt + 2))
                kb.op("pe", lambda: PE.matmul(bank(ob)[:, c0:512], lhsT=Vt[hp][:, kt, :], rhs=pt[:, 0:ncol],
                                              start=(kt == 0), stop=(kt == nk - 1)), r=v_r + [("PT", pi_)], w=[PSR(ob)], inc=False)
                kb.op("pe", lambda: PE.matmul(bank(dbk)[:, c0:512], lhsT=ones_b, rhs=pt[:, 0:ncol],
                                              start=(kt == 0), stop=(kt == nk - 1)), r=["onesb", ("PT", pi_)], w=[PSR(dbk)])
            rd = rden[qb % 2]
            kb.op("dve", lambda: V.reciprocal(out=rd, in_=bank(dbk)), r=[PSR(dbk)], w=[("rden", qb % 2)])
            kb.op("dve", lambda: V.tensor_tensor(out=mixT[:, 8 + h, qb * 512:(qb + 1) * 512], in0=bank(ob), in1=rd, op=ALU.mult),
                  r=[PSR(ob), ("rden", qb % 2)], w=[("mixT", 8 + h, qb)])
    ar.release()
    mix_all = [("mixT", 8 + h, qb) for h in range(MH) for qb in range(4)] + [("mixT", j, c) for j in range(8) for c in range(NT)]
    dump("mixT", mixT, mix_all)
    kb.barrier_all()
    if stop_after == "A3":
        return finish()

    ar.top = ar.marks[0] if False else ar.top
    ar.top = B_BASE[0]
    xr = [ar.alloc([D], F32) for _ in range(4)]
    hT2 = ar.alloc([16, 512], BF16)
    xs2 = [ar.alloc([D], BF16) for _ in range(2)]
    xs2_bufs = [(xs2[0], ("xs2", 0)), (xs2[1], ("xs2", 1))]
    halo2 = ar.alloc([NFB * 2, 2], F32)
    fcw = P_("ffn_cw").rearrange("p (b k) -> p b k", k=3)
    fcb = P_("ffn_cb")

    for s in range(4):
        t0 = s * 512
        ar.mark()
        wblk = [ar.alloc([16, 512], BF16) for _ in range(2)]
        for i in range(4):
            kb.dma("sp", xr[i], x_d[t0 + i * 128:t0 + (i + 1) * 128, :], w=[("xr", i)])
        mix_r = [("mixT", 8 + h, s) for h in range(MH)] + [("mixT", j, s * 4 + i) for j in range(8) for i in range(4)]
        for cb_ in range(4):
            wt, wreg = wblk[cb_ % 2], ("wblk", cb_ % 2)
            kb.dma("sp", wt, wb_out[:, cb_ * 512:(cb_ + 1) * 512].rearrange("(c p) n -> p c n", p=128),
                   r=[("wsc", "out")], w=[wreg])
            for i in range(4):
                b = (cb_ * 4 + i) % 4
                for kc in range(16):
                    kb.op("pe", lambda: PE.matmul(bank(b), lhsT=mixT[:, kc, t0 + i * 128:t0 + (i + 1) * 128], rhs=wt[:, kc, :],
                                                  start=(kc == 0), stop=(kc == 15)), r=mix_r + [wreg], w=[PSR(b)], inc=(kc == 15))
                cs = slice(cb_ * 512, (cb_ + 1) * 512)
                kb.op("dve", lambda: V.tensor_tensor(out=xr[i][:, cs], in0=xr[i][:, cs], in1=bank(b), op=ALU.add),
                      r=[PSR(b), ("xr", i)], w=[("xr", i)])
        if s == 0:
            for i in range(4):
                dump("x1_%d" % i, xr[i], [("xr", i)])
        ar.release()
        kb.barrier_all()
        if stop_after == "B1":
            return finish()
        for half in range(2):
            rms_to_T([(xr[half * 2], ("xr", half * 2)), (xr[half * 2 + 1], ("xr", half * 2 + 1))], D, P_("ffn_nw"), hT2, half * 256,
                     lambda c, half=half: ("hT2", c, half), [4, 5], xs2_bufs)
        hT2_all = [("hT2", c, hh) for c in range(16) for hh in range(2)]
        ar.mark()
        actT = ar.alloc([22, 512], BF16)
        wup = [ar.alloc([2, 16, 128], BF16) for _ in range(3)]
        wdn = [ar.alloc([2, 1024], BF16) for _ in range(3)]
        stg = [ar.alloc([2, 514], F32) for _ in range(2)]
        accg = [ar.alloc([2, 512], F32) for _ in range(2)]
        wu_i = 0
        wd_i = 0
        for hf in range(2):
            for jb in range(22):
                j = hf * 22 + jb
                wi = wu_i % 3
                wu_i += 1
                wt = wup[wi]
                for g in range(2):
                    kb.dma("sp", wt[:, g, :, :], wb_up[:, g * DFF + j * 128:g * DFF + (j + 1) * 128].rearrange("(c p) n -> p c n", p=128),
                           r=[("wsc", "up")], w=[("wup", wi)])
                pb = [(0, 1), (2, 3)][jb % 2]
                for g in range(2):
                    for kc in range(16):
                        kb.op("pe", lambda: PE.matmul(bank(pb[g]), lhsT=wt[:, g, kc, :], rhs=hT2[:, kc, :], start=(kc == 0), stop=(kc == 15)),
                              r=[("wup", wi)] + hT2_all, w=[PSR(pb[g])], inc=(kc == 15))
                st, sreg = stg[jb % 2], ("stg", jb % 2)
                ac, areg = accg[jb % 2], ("accg", jb % 2)
                hreg = ("halo2", j)
                if s == 0:
                    kb.op("pool", lambda: G.memset(st[:, :, 0:2], 0.0), w=[sreg])
                else:
                    kb.op("pool", lambda: G.tensor_copy(out=st[:, :, 0:2], in_=halo2[:, 2 * j:2 * j + 2, :]), r=[hreg], w=[sreg])
                kb.op("act", lambda: A.copy(out=st[:, 0, 2:514], in_=bank(pb[0])), r=[PSR(pb[0])], w=[sreg])
                kb.op("act", lambda: A.copy(out=st[:, 1, 2:514], in_=bank(pb[1])), r=[PSR(pb[1])], w=[sreg])
                kb.op("pool", lambda: G.tensor_copy(out=halo2[:, 2 * j:2 * j + 2, :], in_=st[:, :, 512:514]), r=[sreg], w=[hreg])
                for g in range(2):
                    jj = g * NFB + j
                    kb.op("act", lambda: A.activation(out=ac[:, g, :], in_=st[:, g, 2:514], func=AF.Identity, bias=fcb[:, jj:jj + 1],
                                                      scale=fcw[:, jj, 2:3]), r=[sreg, "prm"], w=[areg])
                    for k in range(2):
                        eng = "dve"
                        E_ = V
                        kb.op(eng, lambda: E_.scalar_tensor_tensor(out=ac[:, g, :], in0=st[:, g, k:k + 512], scalar=fcw[:, jj, k:k + 1],
                                                                    in1=ac[:, g, :], op0=ALU.mult, op1=ALU.add),
                              r=[sreg, areg, "prm"], w=[areg])
                kb.op("act", lambda: A.activation(out=ac[:, 0, :], in_=ac[:, 0, :], func=AF.Silu), r=[areg], w=[areg])
                kb.op("pool", lambda: G.tensor_tensor(out=actT[:, jb, :], in0=ac[:, 0, :], in1=ac[:, 1, :], op=ALU.mult),
                      r=[areg], w=[("actT", jb)])
            act_all = [("actT", jb) for jb in range(22)]
            for ch in range(2):
                for k2 in range(11):
                    wi = wd_i % 3
                    wd_i += 1
                    wt = wdn[wi]
                    r0 = (hf * 22 + k2 * 2) * 128
                    kb.dma("sp", wt, wb_dn[r0:r0 + 256, ch * 1024:(ch + 1) * 1024].rearrange("(c p) n -> p c n", p=128),
                           r=[("wsc", "dn")], w=[("wdn", wi)])
                    for kk in range(2):
                        jb = k2 * 2 + kk
                        for i in range(4):
                            for cb_ in range(2):
                                b = i * 2 + cb_
                                kb.op("pe", lambda: PE.matmul(bank(b), lhsT=actT[:, jb, i * 128:(i + 1) * 128],
                                                              rhs=wt[:, kk, cb_ * 512:(cb_ + 1) * 512], start=(jb == 0), stop=(jb == 21)),
                                      r=[("actT", jb), ("wdn", wi)], w=[PSR(b)], inc=(jb == 21))
                for i in range(4):
                    for cb_ in range(2):
                        b = i * 2 + cb_
                        cs = slice(ch * 1024 + cb_ * 512, ch * 1024 + (cb_ + 1) * 512)
                        kb.op("dve", lambda: V.tensor_tensor(out=xr[i][:, cs], in0=xr[i][:, cs], in1=bank(b), op=ALU.add),
                              r=[PSR(b), ("xr", i)], w=[("xr", i)])
        if s == 0:
            for i in range(4):
                dump("x2_%d" % i, xr[i], [("xr", i)])
        ar.release()
        kb.barrier_all()
        if stop_after == "B2":
            return finish()
        for half in range(2):
            rms_to_T([(xr[half * 2], ("xr", half * 2)), (xr[half * 2 + 1], ("xr", half * 2 + 1))], D, P_("ple_nw"), hT2, half * 256,
                     lambda c, half=half: ("hT2", c, half), [4, 5], xs2_bufs)
        ar.mark()
        wblk = [ar.alloc([16, 512], BF16) for _ in range(2)]
        wpp_sb = ar.alloc([2, D], BF16)
        fnw = ar.alloc([D], F32)
        pnw = ar.alloc([D], F32)
        bhl = ar.alloc([D], BF16)
        pT = ar.alloc([2, 512], BF16)
        pin = [ar.alloc([PLE], F32) for _ in range(2)]
        pinb = [ar.alloc([PLE], BF16) for _ in range(2)]
        tg = [ar.alloc([512], F32) for _ in range(2)]
        tp = [ar.alloc([512], F32) for _ in range(2)]
        kb.dma("sp", wpp_sb, wb_pp.rearrange("(c p) n -> p c n", p=128), r=[("wsc", "pp")], w=["wpp_sb"])
        kb.dma("sp", fnw, fnw_d[:, :], w=["fnw"])
        kb.dma("sp", pnw, pnw_d[:, :], w=["pnw"])
        kb.dma("pool", bhl[0:1, :], bpg_d[0:1, :], w=["bhl"])
        pss = []
        for i in range(4):
            pb_, preg = pin[i % 2], ("pin", i % 2)
            kb.dma("sp", pb_, p_d[t0 + i * 128:t0 + (i + 1) * 128, :], w=[preg])
            kb.op("pool", lambda: G.tensor_copy(out=pinb[i % 2], in_=pb_), r=[preg], w=[("pinb", i % 2)])
            for c in range(2):
                kb.op("pe", lambda: PE.matmul(bank(6)[:, c * 128:(c + 1) * 128], lhsT=pinb[i % 2][:, c * 128:(c + 1) * 128], rhs=ident_b,
                                              start=True, stop=True), r=[("pinb", i % 2), "identb"], w=[PSR(6)], inc=(c == 1))
            kb.op("act", lambda: A.copy(out=pT[:, :, i * 128:(i + 1) * 128], in_=bank(6)[:, 0:256].rearrange("p (c t) -> p c t", c=2)),
                  r=[PSR(6)], w=[("pT", i)])
        for i in range(4):
            ss = small.pop(0)
            small.append(ss)
            k = ("ss", id(ss))
            for cb_ in range(4):
                b = cb_ % 2
                cs = slice(cb_ * 512, (cb_ + 1) * 512)
                for c in range(2):
                    kb.op("pe", lambda: PE.matmul(bank(b), lhsT=pT[:, c, i * 128:(i + 1) * 128], rhs=wpp_sb[:, c, cs],
                                                  start=(c == 0), stop=(c == 1)), r=[("pT", i), "wpp_sb"], w=[PSR(b)], inc=(c == 1))
                kb.op("act", lambda: A.activation(out=junk0[:, 0:512], in_=bank(b), func=AF.Square, accum_out=ss[:, cb_:cb_ + 1]),
                      r=[PSR(b)], w=["junk", k])
            kb.op("dve", lambda: V.tensor_reduce(out=ss[:, 4:5], in_=ss[:, 0:4], axis=mybir.AxisListType.X, op=ALU.add), r=[k], w=[k])
            kb.op("dve", lambda: V.tensor_scalar(out=ss[:, 5:6], in0=ss[:, 4:5], scalar1=1.0 / D, scalar2=EPS, op0=ALU.mult, op1=ALU.add),
                  r=[k], w=[k])
            kb.op("act", lambda: A.sqrt(out=ss[:, 6:7], in_=ss[:, 5:6]), r=[k], w=[k])
            kb.op("dve", lambda: V.reciprocal(out=ss[:, 7:8], in_=ss[:, 6:7]), r=[k], w=[k])
            pss.append((ss, k))
        for cb_ in range(4):
            cs = slice(cb_ * 512, (cb_ + 1) * 512)
            wt, wreg = wblk[cb_ % 2], ("wblk", cb_ % 2)
            kb.dma("sp", wt, wb_pg[:, cs].rearrange("(c p) n -> p c n", p=128), r=[("wsc", "pg")], w=[wreg])
            for i in range(4):
                b = 2 + (i % 2)
                b2 = 4 + (i % 2)
                for kc in range(16):
                    kb.op("pe", lambda: PE.matmul(bank(b), lhsT=hT2[:, kc, i * 128:(i + 1) * 128], rhs=wt[:, kc, :],
                                                  start=(kc == 0), stop=False), r=hT2_all + [wreg], w=[PSR(b)], inc=False)
                kb.op("pe", lambda: PE.matmul(bank(b), lhsT=ones_b[0:1, :], rhs=bhl[0:1, cs], start=False, stop=True),
                      r=["onesb", "bhl"], w=[PSR(b)])
                for c in range(2):
                    kb.op("pe", lambda: PE.matmul(bank(b2), lhsT=pT[:, c, i * 128:(i + 1) * 128], rhs=wpp_sb[:, c, cs],
                                                  start=(c == 0), stop=(c == 1)), r=[("pT", i), "wpp_sb"], w=[PSR(b2)], inc=(c == 1))
                ss, k = pss[i]
                g_, greg = tg[i % 2], ("tg", i % 2)
                p_, pnreg = tp[i % 2], ("tp", i % 2)
                kb.op("act", lambda: A.activation(out=g_, in_=bank(b), func=AF.Sigmoid), r=[PSR(b)], w=[greg])
                kb.op("dve", lambda: V.scalar_tensor_tensor(out=p_, in0=bank(b2), scalar=ss[:, 7:8], in1=pnw[:, cs], op0=ALU.mult, op1=ALU.mult),
                      r=[PSR(b2), k, "pnw"], w=[pnreg])
                kb.op("pool", lambda: G.tensor_tensor(out=p_, in0=p_, in1=g_, op=ALU.mult), r=[pnreg, greg], w=[pnreg])
                kb.op("dve", lambda: V.tensor_tensor(out=xr[i][:, cs], in0=xr[i][:, cs], in1=p_, op=ALU.add), r=[pnreg, ("xr", i)], w=[("xr", i)])
        for i in range(4):
            ss, k = rstd_of(xr[i], D, ("xr", i))
            kb.op("dve", lambda: V.scalar_tensor_tensor(out=xr[i], in0=xr[i], scalar=ss[:, 3:4], in1=fnw, op0=ALU.mult, op1=ALU.mult),
                  r=[("xr", i), k, "fnw"], w=[("xr", i)])
            kb.dma("sp", out_d[t0 + i * 128:t0 + (i + 1) * 128, :], xr[i], r=[("xr", i)], w=[("out", s, i)])
        ar.release()
    return finish()


PRM_OFF = {}
PRM_N = 0


def _layout():
    global PRM_N
    off = 0
    for name, n in [("ident", 128), ("tinc", 128), ("ones", 128), ("negmask", 128), ("invf", 1), ("sgn", 1),
                    ("mix_nw", 16), ("ffn_nw", 16), ("ple_nw", 16), ("qa_nw", 4), ("kv_nw", 2), ("ssd_nw", 8),
                    ("conv_w", 48), ("conv_b", 12), ("dt_bias", 16), ("a_log", 16), ("d_skip", 16),
                    ("ffn_cw", 264), ("ffn_cb", 88)]:
        PRM_OFF[name] = (off, n)
        off += n
    PRM_N = off


_layout()


def _pack_params(inp):
    prm = np.zeros((128, PRM_N), np.float32)

    def put(name, arr):
        o, n = PRM_OFF[name]
        prm[:, o:o + n] = np.asarray(arr, np.float32).reshape(128, n)

    def fm(v):
        v = np.asarray(v, np.float32).reshape(-1)
        return v.reshape(-1, 128).T

    put("ident", np.eye(128, dtype=np.float32))
    put("tinc", np.triu(np.ones((128, 128), np.float32)))
    put("ones", np.ones((128, 128), np.float32))
    put("negmask", -30000.0 * np.tril(np.ones((128, 128), np.float32), -1))
    invf = (10000.0 ** (-np.arange(0, ROPE, 2, dtype=np.float32) / ROPE)).astype(np.float32)
    put("invf", np.tile(invf, 4).reshape(128, 1))
    sgn = np.ones((128, 1), np.float32)
    sgn[0:32] = -1.0
    sgn[64:96] = -1.0
    put("sgn", sgn)
    put("mix_nw", fm(inp["mix_norm_w"]))
    put("ffn_nw", fm(inp["ffn_norm_w"]))
    put("ple_nw", fm(inp["ple_norm_w"]))
    put("qa_nw", fm(inp["q_a_norm_w"]))
    put("kv_nw", fm(inp["kv_a_norm_w"]))
    put("ssd_nw", fm(inp["ssd_norm_w"]))
    cwt = np.asarray(inp["conv_w"], np.float32).reshape(4, 12, 128).transpose(2, 1, 0)
    put("conv_w", cwt.reshape(128, 48))
    put("conv_b", fm(inp["conv_b"]))
    for nm, key in (("dt_bias", "dt_bias"), ("a_log", "a_log"), ("d_skip", "d_skip")):
        put(nm, np.broadcast_to(np.asarray(inp[key], np.float32).reshape(1, 16), (128, 16)))
    fcw = np.asarray(inp["ffn_conv_w"], np.float32).reshape(3, 88, 128).transpose(2, 1, 0)
    put("ffn_cw", fcw.reshape(128, 264))
    put("ffn_cb", fm(inp["ffn_conv_b"]))
    return prm


def _weights(inp):
    w_in = np.asarray(inp["w_in"], np.float32).reshape(D, DIN)
    w_in_ext = np.ascontiguousarray(np.concatenate([w_in, w_in[:, 3376:3408], w_in[:, 3344:3376]], axis=1))
    wq = np.asarray(inp["w_q_b"], np.float32).reshape(QR, MH, 192)
    wq_ext = np.ascontiguousarray(np.concatenate([wq, wq[:, :, 160:192], wq[:, :, 128:160]], axis=2).reshape(QR, MH * 256))
    return {
        "w_in": w_in_ext,
        "w_qb": wq_ext,
        "w_kvb": np.ascontiguousarray(np.asarray(inp["w_kv_b"], np.float32).reshape(KVR, 2048)),
        "w_out": np.ascontiguousarray(np.asarray(inp["w_out"], np.float32).reshape(D, D)),
        "w_up": np.ascontiguousarray(np.asarray(inp["w_ffn_up"], np.float32).reshape(D, 2 * DFF)),
        "w_dn": np.ascontiguousarray(np.asarray(inp["w_ffn_down"], np.float32).reshape(DFF, D)),
        "w_pg": np.ascontiguousarray(np.asarray(inp["w_ple_gate"], np.float32).reshape(D, D)),
        "w_pp": np.ascontiguousarray(np.asarray(inp["w_ple_proj"], np.float32).reshape(PLE, D)),
    }


def make_in_maps(inp, cores):
    prm = _pack_params(inp)
    wts = _weights(inp)
    x = np.asarray(inp["x"], np.float32)
    p = np.asarray(inp["p"], np.float32)
    pos = np.asarray(inp["positions"], np.int32)
    fnw_b = np.ascontiguousarray(np.broadcast_to(np.asarray(inp["final_norm_w"], np.float32).reshape(1, D), (128, D)))
    pnw_b = np.ascontiguousarray(np.broadcast_to(np.asarray(inp["ple_post_norm_w"], np.float32).reshape(1, D), (128, D)))
    bpg = np.ascontiguousarray(np.broadcast_to(np.asarray(inp["b_ple_gate"], np.float32).reshape(1, D), (2, D)))
    maps = []
    for b in cores:
        m = {"x": np.ascontiguousarray(x[b]), "p": np.ascontiguousarray(p[0, b]),
             "pos": np.ascontiguousarray(np.broadcast_to(pos[b].reshape(1, S), (64, S))), "prm": prm,
             "fnw_b": fnw_b, "pnw_b": pnw_b, "bpg": bpg}
        m.update(wts)
        maps.append(m)
    return maps


def kernel(**inputs):
    nc, _ = build_nc()
    in_maps = make_in_maps(inputs, list(range(8)))
    res = run_bass_kernel_spmd(nc, in_maps, core_ids=list(range(8)))
    return np.stack([np.asarray(r["out"], np.float32) for r in res.results], axis=0)
```
